# Optimizing a Trainium2 kernel written in Bass

```python
import jax, jax.numpy as jnp
from jax import lax
import numpy as np

D_MODEL = 1024
BATCH = 2
SEQ = 8192
DEPTH = 2

LRU_WIDTH = D_MODEL
LRU_BLOCKS = 16
LRU_BW = LRU_WIDTH // LRU_BLOCKS
CONV_W = 4
LRU_C = 8.0
DIL_PATTERNS = ((128, 1), (512, 4), (2048, 16))
N_GROUPS = 3
HEADS_PER_GROUP = 8
HEAD_DIM = 64
ATT_WIDTH = N_GROUPS * HEADS_PER_GROUP * HEAD_DIM
ATT_OUT = HEADS_PER_GROUP * HEAD_DIM
BAND_BLK = 128
ROPE_THETA = 10000.0
GLA_HEADS = 4
GLA_DK = D_MODEL // 2 // GLA_HEADS
GLA_DV = D_MODEL // GLA_HEADS
GLA_KW = GLA_HEADS * GLA_DK
GLA_VW = GLA_HEADS * GLA_DV
GLA_RANK = 16
GLA_NORMALIZER = 16.0
GLA_CHUNK = 64
N_BRANCH = 3
EPS = 1e-6
IN_SIZES = (LRU_WIDTH, LRU_WIDTH, ATT_WIDTH, ATT_WIDTH, ATT_WIDTH, ATT_OUT,
            GLA_KW, GLA_KW, GLA_VW, GLA_VW, N_BRANCH * D_MODEL)
N_IN = sum(IN_SIZES)

kernel_name = 'hybrid_rglru_dilatedswa_gla_parallel_gated'


def rmsnorm(x, g):
    xf = x.astype(jnp.float32)
    y = xf * lax.rsqrt(jnp.mean(xf * xf, axis=-1, keepdims=True) + EPS)
    return (y * g.astype(jnp.float32)).astype(x.dtype)


def rope(x, pos):
    half = x.shape[-1] // 2
    inv = ROPE_THETA ** (-(jnp.arange(half, dtype=jnp.float32) / half))
    ang = pos.astype(jnp.float32)[:, :, None] * inv
    ang = ang.reshape(ang.shape[:2] + (1,) * (x.ndim - 3) + (half,))
    cos, sin = jnp.cos(ang), jnp.sin(ang)
    xf = x.astype(jnp.float32)
    x1, x2 = xf[..., :half], xf[..., half:]
    return jnp.concatenate([x1 * cos - x2 * sin, x2 * cos + x1 * sin], axis=-1).astype(x.dtype)


def causal_dwconv(x, w, b):
    K = w.shape[0]
    S = x.shape[1]
    xp = jnp.pad(x, ((0, 0), (K - 1, 0), (0, 0)))
    y = b + xp[:, 0:S] * w[0]
    for k in range(1, K):
        y = y + xp[:, k:k + S] * w[k]
    return y


def rg_lru(x, w_a, b_a, w_x, b_x, lam):
    B, S, W = x.shape
    xf = x.astype(jnp.float32)
    xb = xf.reshape(B, S, LRU_BLOCKS, LRU_BW)
    r = jax.nn.sigmoid(jnp.einsum('bsnj,njk->bsnk', xb, w_a.astype(jnp.float32)).reshape(B, S, W) + b_a)
    i = jax.nn.sigmoid(jnp.einsum('bsnj,njk->bsnk', xb, w_x.astype(jnp.float32)).reshape(B, S, W) + b_x)
    log_a = -LRU_C * r * jax.nn.softplus(-lam.astype(jnp.float32))
    a = jnp.exp(log_a)
    u = jnp.sqrt(-jnp.expm1(2.0 * log_a)) * (i * xf)

    def combine(e1, e2):
        a1, b1 = e1
        a2, b2 = e2
        return a1 * a2, a2 * b1 + b2

    _, h = lax.associative_scan(combine, (a, u), axis=1)
    return h.astype(x.dtype)


def dilated_group(q, k, v, window, dil):
    B, S, H, dh = q.shape
    span = window // dil
    unit = dil * BAND_BLK
    pad = (-S) % unit
    L = (S + pad) // dil
    nb = L // BAND_BLK

    def to_blocks(t):
        t = jnp.pad(t, ((0, 0), (0, pad), (0, 0), (0, 0)))
        t = t.reshape(B, L, dil, H, dh).transpose(0, 2, 1, 3, 4)
        return t.reshape(B, dil, nb, BAND_BLK, H, dh)

    def band_keys(t):
        prev = jnp.pad(t, ((0, 0), (0, 0), (1, 0), (0, 0), (0, 0), (0, 0)))[:, :, :-1]
        return jnp.concatenate([prev, t], axis=3)

    qb = to_blocks(q)
    kc = band_keys(to_blocks(k))
    vc = band_keys(to_blocks(v)).astype(jnp.float32)
    s = jnp.einsum('bdnqhe,bdnkhe->bdnhqk', qb, kc, preferred_element_type=jnp.float32)
    qi = jnp.arange(BAND_BLK)[:, None]
    kj = jnp.arange(2 * BAND_BLK)[None, :]
    rel = qi + BAND_BLK - kj
    band = (rel >= 0) & (rel <= span)
    has_prev = (jnp.arange(nb) > 0)[:, None, None] | (kj >= BAND_BLK)[None]
    mask = band[None] & has_prev
    s = jnp.where(mask[:, None], s, -jnp.inf)
    m = jnp.max(s, axis=-1, keepdims=True)
    p = jnp.exp(s - m)
    l = jnp.sum(p, axis=-1, keepdims=True)
    o = jnp.einsum('bdnhqk,bdnkhe->bdnqhe', p / l, vc)
    lse = jnp.swapaxes((m + jnp.log(l))[..., 0], 3, 4)
    o = o.reshape(B, dil, L, H, dh).transpose(0, 2, 1, 3, 4).reshape(B, L * dil, H, dh)[:, :S]
    lse = lse.reshape(B, dil, L, H).transpose(0, 2, 1, 3).reshape(B, L * dil, H)[:, :S]
    return o, lse


def gla_chunked(q, k, v, log_alpha):
    B, S, H, dk = q.shape
    dv = v.shape[-1]
    C = GLA_CHUNK
    n = S // C
    f32 = jnp.float32
    q = q.astype(f32).reshape(B, n, C, H, dk) * (dk ** -0.5)
    k = k.astype(f32).reshape(B, n, C, H, dk)
    v = v.astype(f32).reshape(B, n, C, H, dv)
    b = jnp.cumsum(log_alpha.astype(f32).reshape(B, n, C, H, dk), axis=2)
    b_last = b[:, :, -1:]
    q_dec = q * jnp.exp(b)
    k_inv = k * jnp.exp(-b)
    k_end = k * jnp.exp(b_last - b)
    causal = jnp.tril(jnp.ones((C, C), dtype=bool))
    att = jnp.where(causal, jnp.einsum('bnqhk,bnshk->bnhqs', q_dec, k_inv), 0.0)
    o = jnp.einsum('bnhqs,bnshv->bnqhv', att, v)
    kv = jnp.einsum('bnshk,bnshv->nbhkv', k_end, v)
    decay = jnp.moveaxis(jnp.exp(b_last[:, :, 0]), 1, 0)

    def step(state, inp):
        dec, kv_c = inp
        return dec[..., None] * state + kv_c, state

    _, prev = lax.scan(step, jnp.zeros((B, H, dk, dv), f32), (decay, kv))
    o = o + jnp.einsum('bnqhk,nbhkv->bnqhv', q_dec, prev)
    return o.reshape(B, S, H, dv)


def hybrid_layer(x, c_act, pos, ada_w, ada_b, norm_g, w_in, conv_w, conv_b,
                 lru_wa, lru_ba, lru_wx, lru_bx, lru_lambda, qn_g, kn_g,
                 gla_a1, gla_a2, gla_ab, gla_on_g, proj_a, proj_b, proj_c, w_o):
    B, S, D = x.shape
    mod = c_act @ ada_w + ada_b
    shift, scale, gate = jnp.split(mod, 3, axis=-1)
    h = rmsnorm(x, norm_g) * (1.0 + scale[:, None]) + shift[:, None]
    z = h @ w_in
    (a_x, a_g, b_q, b_k, b_v, b_g, c_q, c_k, c_v, c_g, m_g) = jnp.split(
        z, np.cumsum(IN_SIZES)[:-1].tolist(), axis=-1)

    ya = rg_lru(causal_dwconv(a_x, conv_w, conv_b), lru_wa, lru_ba, lru_wx, lru_bx, lru_lambda)
    ya = ya * jax.nn.silu(a_g)

    shp = (B, S, N_GROUPS, HEADS_PER_GROUP, HEAD_DIM)
    q = rope(rmsnorm(b_q.reshape(shp), qn_g[:, None, :]), pos) * (HEAD_DIM ** -0.5)
    k = rope(rmsnorm(b_k.reshape(shp), kn_g[:, None, :]), pos)
    v = b_v.reshape(shp)
    outs, lses = [], []
    for g, (win, dil) in enumerate(DIL_PATTERNS):
        o_g, lse_g = dilated_group(q[:, :, g], k[:, :, g], v[:, :, g], win, dil)
        outs.append(o_g)
        lses.append(lse_g)
    wts = jax.nn.softmax(jnp.stack(lses, axis=0), axis=0)
    yb = jnp.sum(wts[..., None] * jnp.stack(outs, axis=0), axis=0)
    yb = yb.reshape(B, S, ATT_OUT).astype(x.dtype) * jax.nn.silu(b_g)

    log_alpha = jax.nn.log_sigmoid(((h @ gla_a1) @ gla_a2 + gla_ab).astype(jnp.float32)) / GLA_NORMALIZER
    yc = gla_chunked(c_q.reshape(B, S, GLA_HEADS, GLA_DK), c_k.reshape(B, S, GLA_HEADS, GLA_DK),
                     c_v.reshape(B, S, GLA_HEADS, GLA_DV), log_alpha.reshape(B, S, GLA_HEADS, GLA_DK))
    yc = rmsnorm(yc, gla_on_g).reshape(B, S, GLA_VW).astype(x.dtype) * jax.nn.silu(c_g)

    mg = jax.nn.sigmoid(m_g).reshape(B, S, N_BRANCH, D)
    merged = mg[:, :, 0] * (ya @ proj_a) + mg[:, :, 1] * (yb @ proj_b) + mg[:, :, 2] * (yc @ proj_c)
    return x + gate[:, None] * (merged @ w_o)


def setup_inputs(seed: int = 0) -> dict:
    key = jax.random.key(seed)
    ks = jax.random.split(key, 24)
    D = D_MODEL
    nrm = lambda k, shp, s: jax.random.normal(k, shp, jnp.float32) * s
    x = nrm(ks[0], (BATCH, SEQ, D), 1.0)
    c = nrm(ks[1], (BATCH, D), 1.0)
    positions = (jnp.arange(SEQ, dtype=jnp.int32)[None, :]
                 + jax.random.randint(ks[2], (BATCH, 1), 0, 4096, dtype=jnp.int32))
    ada_w = nrm(ks[3], (DEPTH, D, 3 * D), 0.2 * D ** -0.5)
    ada_b = jnp.concatenate([nrm(ks[4], (DEPTH, 2 * D), 0.02),
                             1.0 + nrm(ks[5], (DEPTH, D), 0.02)], axis=-1)
    norm_g = 1.0 + nrm(ks[6], (DEPTH, D), 0.02)
    w_in = nrm(ks[7], (DEPTH, D, N_IN), D ** -0.5)
    conv_w = nrm(ks[8], (DEPTH, CONV_W, LRU_WIDTH), CONV_W ** -0.5)
    conv_b = nrm(ks[9], (DEPTH, LRU_WIDTH), 0.02)
    lru_wa = nrm(ks[10], (DEPTH, LRU_BLOCKS, LRU_BW, LRU_BW), LRU_BW ** -0.5)
    lru_ba = nrm(ks[11], (DEPTH, LRU_WIDTH), 0.02)
    lru_wx = nrm(ks[12], (DEPTH, LRU_BLOCKS, LRU_BW, LRU_BW), LRU_BW ** -0.5)
    lru_bx = nrm(ks[13], (DEPTH, LRU_WIDTH), 0.02)
    u = jax.random.uniform(ks[14], (DEPTH, LRU_WIDTH), jnp.float32, 0.9, 0.999)
    a0 = u ** (1.0 / LRU_C)
    lru_lambda = jnp.log(a0) - jnp.log1p(-a0)
    qn_g = 1.0 + nrm(ks[15], (DEPTH, N_GROUPS, HEAD_DIM), 0.02)
    kn_g = 1.0 + nrm(ks[16], (DEPTH, N_GROUPS, HEAD_DIM), 0.02)
    gla_a1 = nrm(ks[17], (DEPTH, D, GLA_RANK), D ** -0.5)
    gla_a2 = nrm(ks[18], (DEPTH, GLA_RANK, GLA_KW), GLA_RANK ** -0.5)
    gla_ab = nrm(ks[19], (DEPTH, GLA_KW), 0.02)
    gla_on_g = 1.0 + nrm(ks[20], (DEPTH, GLA_DV), 0.02)
    proj_a = nrm(ks[21], (DEPTH, LRU_WIDTH, D), LRU_WIDTH ** -0.5)
    proj_b = nrm(ks[22], (DEPTH, ATT_OUT, D), ATT_OUT ** -0.5)
    kk = jax.random.split(ks[23], 2)
    proj_c = nrm(kk[0], (DEPTH, GLA_VW, D), GLA_VW ** -0.5)
    w_o = nrm(kk[1], (DEPTH, D, D), D ** -0.5)
    return {'x': x, 'c': c, 'positions': positions, 'ada_w': ada_w, 'ada_b': ada_b,
            'norm_g': norm_g, 'w_in': w_in, 'conv_w': conv_w, 'conv_b': conv_b,
            'lru_wa': lru_wa, 'lru_ba': lru_ba, 'lru_wx': lru_wx, 'lru_bx': lru_bx,
            'lru_lambda': lru_lambda, 'qn_g': qn_g, 'kn_g': kn_g, 'gla_a1': gla_a1,
            'gla_a2': gla_a2, 'gla_ab': gla_ab, 'gla_on_g': gla_on_g, 'proj_a': proj_a,
            'proj_b': proj_b, 'proj_c': proj_c, 'w_o': w_o}


def reference(x, c, positions, ada_w, ada_b, norm_g, w_in, conv_w, conv_b,
              lru_wa, lru_ba, lru_wx, lru_bx, lru_lambda, qn_g, kn_g,
              gla_a1, gla_a2, gla_ab, gla_on_g, proj_a, proj_b, proj_c, w_o):
    c_act = jax.nn.silu(c)
    for l in range(DEPTH):
        x = hybrid_layer(x, c_act, positions, ada_w[l], ada_b[l], norm_g[l], w_in[l],
                         conv_w[l], conv_b[l], lru_wa[l], lru_ba[l], lru_wx[l], lru_bx[l],
                         lru_lambda[l], qn_g[l], kn_g[l], gla_a1[l], gla_a2[l], gla_ab[l],
                         gla_on_g[l], proj_a[l], proj_b[l], proj_c[l], w_o[l])
    return x
```

```python
from contextlib import ExitStack
import types
import numpy as np
import ml_dtypes
import concourse.bass as bass
import concourse.mybir as mybir
from concourse.bass_utils import run_bass_kernel_spmd

F32 = mybir.dt.float32
BF16 = mybir.dt.bfloat16
I32 = mybir.dt.int32
ALU = mybir.AluOpType
AF = mybir.ActivationFunctionType

NCORE = 8
D = 1024
T = 2048
NT = 4
TT = 512
DEPTH = 2
EPS = 1e-6
NIN = 13312
O_AX, O_AG, O_BQ, O_BK, O_BV, O_BG, O_CQ, O_CK, O_CV, O_CG, O_MG = (
    0, 1024, 2048, 3584, 5120, 6656, 7168, 7680, 8192, 9216, 10240)
DILS = (1, 4, 16)
KVW = 4 * 2 * 128 * 21
FINW = 32 + 1024
WSIZES = (("w_in", D * NIN), ("ada_w", D * 3 * D), ("proj_a", D * D), ("proj_b", 512 * D), ("proj_c", D * D), ("w_o", D * D))
WOFF = {}
_o = 0
for _n, _s in WSIZES:
    WOFF[_n] = _o
    _o += _s
WPER = _o
WROWS = DEPTH * WPER // 2048
assert DEPTH * WPER % (2048 * NCORE) == 0
NV = 108
ENGS = ("sp", "act", "dve", "pool", "pe")


def kv_off(g, hp, kv):
    off = 0
    for gg in range(g):
        off += 4 * 2 * 128 * DILS[gg]
    return off + (hp * 2 + kv) * 128 * DILS[g]


def _freeze(fn):
    if getattr(fn, "__closure__", None) is None:
        return fn
    cells = []
    for c in fn.__closure__:
        try:
            cells.append(types.CellType(c.cell_contents))
        except ValueError:
            cells.append(c)
    return types.FunctionType(fn.__code__, fn.__globals__, fn.__name__, fn.__defaults__, tuple(cells))


class Buf:
    __slots__ = ("name", "h", "w", "r")

    def __init__(self, name, h):
        self.name = name
        self.h = h
        self.w = {}
        self.r = {}

    def __getitem__(self, idx):
        return self.h[idx]


class Prog:
    NDMASEM = 16

    def __init__(self, nc):
        self.nc = nc
        self.q = {e: [] for e in ENGS}
        self.cnt = {e: 0 for e in ENGS}
        self.marks = {e: set() for e in ENGS}
        self.waited = {e: {} for e in ENGS}
        self.sem = {e: nc.alloc_semaphore(name=f"c_{e}") for e in ENGS}
        self.dpool, self.dpos, self.dcnt = {}, {}, {}
        for e in ("sp", "act", "pool"):
            self.dpool[e] = [nc.alloc_semaphore(name=f"d_{e}{i}") for i in range(self.NDMASEM)]
            self.dpos[e] = 0
            self.dcnt[e] = [0] * self.NDMASEM
        self.dynoff = None
        self.idx_ap = None
        self.nsb = 0

    def sb(self, es, name, shape, dt):
        self.nsb += 1
        name = f"s{self.nsb}_{name}"
        return Buf(name, es.enter_context(self.nc.sbuf_tensor(name, list(shape), dt)))

    def ps(self, name, shape, dt=F32):
        return Buf(name, self.nc.alloc_psum_tensor(name, list(shape), dt))

    def dram(self, name, shape, dt, kind="Internal"):
        return Buf(name, self.nc.dram_tensor(name, list(shape), dt, kind=kind))

    def _wait(self, eng, tok):
        kind, key, val = tok
        if kind == "c":
            if key == "pe" and eng == "pe":
                return
            w = self.waited[eng]
            if w.get(key, 0) >= val:
                return
            w[key] = val
            self.marks[key].add(val)
            self.q[eng].append(("wc", key, val))
        else:
            w = self.waited[eng]
            k = id(key)
            if w.get(k, 0) >= val:
                return
            w[k] = val
            self.q[eng].append(("wd", key, val))

    def _deps(self, eng, reads, writes):
        for b in reads:
            for tok in b.w.values():
                self._wait(eng, tok)
        for b in writes:
            for tok in b.w.values():
                self._wait(eng, tok)
            for tok in b.r.values():
                self._wait(eng, tok)

    def _commit(self, tok, reads, writes):
        k = tok[1] if tok[0] == "c" else id(tok[1])
        for b in writes:
            b.w = {k: tok}
            b.r = {}
        for b in reads:
            if b in writes:
                continue
            old = b.r.get(k)
            if old is None or old[2] < tok[2]:
                b.r[k] = tok

    def op(self, eng, fn, reads=(), writes=()):
        self._deps(eng, reads, writes)
        self.cnt[eng] += 1
        n = self.cnt[eng]
        self.q[eng].append(("op", _freeze(fn), n))
        self._commit(("c", eng, n), reads, writes)

    def dma(self, eng, fns, reads=(), writes=()):
        if not isinstance(fns, (list, tuple)):
            fns = [fns]
        self._deps(eng, reads, writes)
        i = self.dpos[eng]
        self.dpos[eng] = (i + 1) % self.NDMASEM
        sem = self.dpool[eng][i]
        prev = self.dcnt[eng][i]
        if prev:
            self._wait(eng, ("d", sem, prev))
        val = prev + 16 * len(fns)
        self.dcnt[eng][i] = val
        for fn in fns:
            self.q[eng].append(("dma", _freeze(fn), sem))
        self._commit(("d", sem, val), reads, writes)

    def coll(self, fn, reads=(), writes=(), key="w"):
        if not hasattr(self, "csem"):
            self.csem, self.ccnt = {}, {}
        if key not in self.csem:
            self.csem[key] = self.nc.alloc_semaphore(name=f"c_coll_{key}")
            self.ccnt[key] = 0
        self._deps("pool", reads, writes)
        if self.ccnt[key]:
            self._wait("pool", ("d", self.csem[key], self.ccnt[key]))
        self.ccnt[key] += 1
        self.q["pool"].append(("coll", _freeze(fn), self.csem[key]))
        self._commit(("d", self.csem[key], self.ccnt[key]), reads, writes)

    def barrier(self, coll=True):
        toks = [("c", e, self.cnt[e]) for e in ENGS if self.cnt[e]]
        if coll:
            for key, v in getattr(self, "ccnt", {}).items():
                if v:
                    toks.append(("d", self.csem[key], v))
        for e in ("sp", "act", "pool"):
            for i, v in enumerate(self.dcnt[e]):
                if v:
                    toks.append(("d", self.dpool[e][i], v))
        for e in ENGS:
            for tok in toks:
                if tok[0] == "c" and tok[1] == e:
                    continue
                self._wait(e, tok)

    def _replay(self, eng, e):
        rank = {}
        for E in ENGS:
            rank[E] = {idx: i + 1 for i, idx in enumerate(sorted(self.marks[E]))}
        mine = self.marks[eng]
        for item in self.q[eng]:
            kind = item[0]
            if kind == "op":
                ins = item[1](e)
                if item[2] in mine:
                    ins.then_inc(self.sem[eng], 1)
            elif kind == "wc":
                e.wait_ge(self.sem[item[1]], rank[item[1]][item[2]])
            elif kind == "wd":
                e.wait_ge(item[1], item[2])
            elif kind == "dma":
                item[1](e).then_inc(item[2], 16)
            elif kind == "coll":
                item[1](e).then_inc(item[2], 1)

    def emit(self):
        nc = self.nc
        with nc.Block() as block:
            @block.sync
            def _(e):
                self._replay("sp", e)

            @block.scalar
            def _(e):
                self._replay("act", e)

            @block.vector
            def _(e):
                self._replay("dve", e)

            @block.gpsimd
            def _(e):
                self._replay("pool", e)

            @block.tensor
            def _(e):
                self._replay("pe", e)


def build_program(debug=None, layers=(0, 1)):
    nc = bass.Bass("TRN2", target_bir_lowering=False)
    P = Prog(nc)
    EI = "ExternalInput"
    xT_d = P.dram("xT", [D, T], F32, EI)
    xh0_d = P.dram("xh0", [128, 32], F32, EI)
    pos_d = P.dram("posb", [128, T], I32, EI)
    cT_d = P.dram("cT", [128, 8], F32, EI)
    flags_d = P.dram("flags", [128, 24], F32, EI)
    idx_d = P.dram("idxp", [1, 1], I32, EI)
    negm_d = P.dram("negm", [128, 512], BF16, EI)
    cbf_d = P.dram("cbf", [128, 640 + 2 * T], BF16, EI)
    cf_d = P.dram("cf", [128, 3 * 128 + 4], F32, EI)
    vec_d = P.dram("vec", [DEPTH, 128, NV], F32, EI)
    lru_wa_d = P.dram("lru_wa", [DEPTH, 16, 64, 64], F32, EI)
    lru_wx_d = P.dram("lru_wx", [DEPTH, 16, 64, 64], F32, EI)
    a1_d = P.dram("gla_a1", [DEPTH, D, 16], F32, EI)
    a2_d = P.dram("gla_a2", [DEPTH, 16, 512], F32, EI)
    wshard_d = P.dram("wshard", [WROWS // NCORE, 2048], F32, EI)
    LR = WROWS // DEPTH
    wfull_l = [P.dram(f"wfull{l}", [LR, 2048], F32) for l in range(DEPTH)]
    wbounce_l = [P.dram(f"wbounce{l}", [LR // NCORE, 2048], F32) for l in range(DEPTH)]

    class WView:
        def __init__(self, name, rows, cols):
            self.name, self.rows, self.cols = name, rows, cols
            self.h = self

        def ap(self):
            return self

        def __getitem__(self, idx):
            l, rs, cs = idx
            off = WOFF[self.name]
            flat = wfull_l[l].h.ap().rearrange("a b -> (a b)")[off:off + self.rows * self.cols]
            return flat.rearrange("(r c) -> r c", c=self.cols)[rs, cs]

    w_in_d = WView("w_in", D, NIN)
    ada_w_d = WView("ada_w", D, 3 * D)
    proj_a_d = WView("proj_a", D, D)
    proj_b_d = WView("proj_b", 512, D)
    proj_c_d = WView("proj_c", D, D)
    w_o_d = WView("w_o", D, D)
    yT_d = P.dram("yT", [D, T], F32, "ExternalOutput")
    P.idx_ap = idx_d[0:1, 0:1]
    dbg_kind = "Internal"
    x1_d = P.dram("x1", [D, T], F32, "ExternalOutput" if debug else "Internal")
    ya_d = P.dram("ya_s", [D, T], BF16, dbg_kind)
    ag_d = P.dram("ag_s", [D, T], BF16, dbg_kind)
    q_d = P.dram("q_s", [12, 128, T], BF16, dbg_kind)
    k_d = P.dram("k_s", [12, 128, T], BF16, dbg_kind)
    v_d = P.dram("v_s", [12, 128, T], BF16, dbg_kind)
    qg_d = P.dram("qg_s", [512, T], BF16, dbg_kind)
    ol_d = P.dram("ol_s", [D, T], F32, dbg_kind)
    tk_d = [P.dram(f"tk{i}_s", [D, T], F32, dbg_kind) for i in range(3)]
    kvsend = P.dram("kvsend", [128, KVW], BF16)
    kvrecv = P.dram("kvrecv", [NCORE * 128, KVW], BF16)
    kvprev = P.dram("kvprev", [128, KVW], BF16)
    finsend = P.dram("finsend", [128, FINW], F32)
    finrecv = P.dram("finrecv", [NCORE * 128, FINW], F32)
    xhsend = P.dram("xhsend", [128, 32], F32)
    xhrecv = P.dram("xhrecv", [NCORE * 128, 32], F32)
    dbg = {}
    if debug:
        dbg["ybT"] = P.dram("dbg_ybT", [512, T], BF16, "ExternalOutput")
        dbg["yaT"] = P.dram("dbg_yaT", [D, T], BF16, "ExternalOutput")
        dbg["ycT"] = P.dram("dbg_ycT", [D, T], BF16, "ExternalOutput")
        dbg["hin"] = P.dram("dbg_hin", [128, 8], F32, "ExternalOutput")
        dbg["finr"] = P.dram("dbg_finr", [128, 256], F32, "ExternalOutput")

    PSB = [P.ps(f"psb{i}", [128, 512], F32) for i in range(7)]
    PSH = P.ps("psh", [128, 1024], BF16)
    bank_i = [0]

    def bank():
        b = PSB[bank_i[0] % 7]
        bank_i[0] += 1
        return b

    def MM(out, lhsT, rhs, start, stop):
        return lambda e: e.matmul(out, lhsT, rhs, start=start, stop=stop)

    def ACT(out, in_, func, bias=None, scale=None):
        kw = {}
        if bias is not None:
            kw["bias"] = bias
        if scale is not None:
            kw["scale"] = scale
        return lambda e: e.activation(out, in_, func, **kw)

    def TT_(out, a, b, op):
        return lambda e: e.tensor_tensor(out, a, b, op)

    def TS(out, a, s1, s2, op0, op1=None):
        if op1 is None:
            return lambda e: e.tensor_scalar(out, a, s1, None, op0)
        return lambda e: e.tensor_scalar(out, a, s1, s2, op0, op1)

    def STT(out, a, s, b, op0, op1):
        return lambda e: e.scalar_tensor_tensor(out, a, s, b, op0, op1)

    def CP(out, in_):
        return lambda e: e.tensor_copy(out, in_)

    def DMA(out, in_):
        return lambda e: e.dma_start(out=out, in_=in_)

    class Ring:
        def __init__(self, es, name, shape, dt, n):
            self.b = [P.sb(es, f"{name}{i}", shape, dt) for i in range(n)]
            self.i = 0

        def next(self):
            b = self.b[self.i % len(self.b)]
            self.i += 1
            return b

    with ExitStack() as glob:
        cbf = P.sb(glob, "cbf", [128, 640 + 2 * T], BF16)
        cf = P.sb(glob, "cf", [128, 388], F32)
        flags = P.sb(glob, "flags", [128, 24], F32)
        negm = P.sb(glob, "negm", [128, 512], BF16)
        vec = P.sb(glob, "vec", [128, DEPTH, NV], F32)
        hT = P.sb(glob, "hT", [128, 8, T], BF16)
        hTh = P.sb(glob, "hTh", [128, 8, 4], BF16)
        der = P.sb(glob, "der", [128, DEPTH, 64], F32)
        P.dma("sp", DMA(cbf[:], cbf_d[:]), [cbf_d], [cbf])
        P.dma("sp", DMA(cf[:], cf_d[:]), [cf_d], [cf])
        P.dma("sp", DMA(flags[:], flags_d[:]), [flags_d], [flags])
        P.dma("sp", DMA(negm[:], negm_d[:]), [negm_d], [negm])
        P.dma("sp", DMA(vec[:], vec_d.h.ap().rearrange("l p n -> p l n")), [vec_d], [vec])
        WR = LR // NCORE
        for l in range(DEPTH):
            P.dma("sp", [DMA(wbounce_l[l][i * (WR // 4):(i + 1) * (WR // 4), :], wshard_d[l * WR + i * (WR // 4):l * WR + (i + 1) * (WR // 4), :]) for i in range(4)],
                  [wshard_d], [wbounce_l[l]])
            P.coll(lambda e, l=l: e.collective_compute("AllGather", ALU.bypass, replica_groups=[list(range(NCORE))],
                                                       ins=[wbounce_l[l].h.ap().opt()], outs=[wfull_l[l].h.ap().opt()]), [wbounce_l[l]], [wfull_l[l]], key=f"w{l}")
        ident = cbf[:, 0:128]
        rperm = cbf[:, 128:256]
        maskpair = cbf[:, 256:384]
        onesb = cbf[:, 384:512]
        resetm = cbf[:, 640:640 + T]
        onesT = cbf[:, 640 + T:640 + 2 * T]
        mean1024 = cf[:, 0:128]
        mean64 = cf[:, 128:256]
        mean256 = cf[:, 256:384]

        def V(l, a, b):
            return vec[:, l, a:b]

        with ExitStack() as es:
            cact = P.sb(es, "cact", [128, 8], F32)
            mod = P.sb(es, "mod", [128, 24], F32)
            tmpc = P.sb(es, "tmpc", [128, 8], F32)
            P.dma("sp", DMA(cact[:], cT_d[:]), [cT_d], [cact])
            P.op("act", ACT(cact[:], cact[:], AF.Silu), [cact], [cact])
            aw = P.sb(es, "aw", [128, 8, 1024], F32)
            for l in range(DEPTH):
                for j3 in range(3):
                    P.dma("sp", [DMA(aw[:, 0:4, :], ada_w_d.h.ap()[l, 0:512, j3 * 1024:(j3 + 1) * 1024].rearrange("(k p) n -> p k n", p=128)),
                                 DMA(aw[:, 4:8, :], ada_w_d.h.ap()[l, 512:1024, j3 * 1024:(j3 + 1) * 1024].rearrange("(k p) n -> p k n", p=128))],
                          [wfull_l[l]], [aw])
                    pb = bank()
                    for j in range(8):
                        for k in range(8):
                            P.op("pe", MM(pb[:, j:j + 1], aw[:, k, j * 128:(j + 1) * 128], cact[:, k:k + 1], k == 0, k == 7),
                                 [aw, cact], [pb])
                    P.op("dve", TT_(mod[:, j3 * 8:(j3 + 1) * 8], pb[:, 0:8], V(l, 84 + j3 * 8, 92 + j3 * 8), ALU.add), [pb, vec], [mod])
                P.op("dve", TS(tmpc[:], mod[:, 8:16], 1.0, None, ALU.add), [mod], [tmpc])
                P.op("dve", TT_(der[:, l, 0:8], tmpc[:], V(l, 0, 8), ALU.mult), [tmpc, vec], [der])
                P.op("dve", CP(der[:, l, 8:16], mod[:, 0:8]), [mod], [der])
                P.op("dve", CP(der[:, l, 16:24], mod[:, 16:24]), [mod], [der])
                P.op("act", ACT(tmpc[:], V(l, 64, 72), AF.Exp, scale=-1.0), [vec], [tmpc])
                P.op("act", ACT(tmpc[:], tmpc[:], AF.Ln, bias=1.0), [tmpc], [tmpc])
                P.op("dve", TS(der[:, l, 24:32], tmpc[:], -8.0, None, ALU.mult), [tmpc], [der])
                P.op("dve", TS(der[:, l, 32:40], tmpc[:], -16.0, None, ALU.mult), [tmpc], [der])
                P.op("dve", TS(der[:, l, 40:43], V(l, 72, 75), 0.125, None, ALU.mult), [vec], [der])
                P.op("dve", TS(der[:, l, 44:48], V(l, 78, 82), -1.0, None, ALU.mult), [vec], [der])
            P.barrier(coll=False)

        WREADS = []

        class Slabs:
            def __init__(self, es, nst=1, nbf=2, tag="", st=None):
                self.st = st if st is not None else [P.sb(es, f"wst{tag}{i}", [128, 8, 512], F32) for i in range(nst)]
                self.bf = [P.sb(es, f"wbf{tag}{i}", [128, 8, 512], BF16) for i in range(nbf)]
                self.si = 0
                self.bi = 0

            def load(self, src_ap, nk=8, width=512, dma_eng="sp"):
                st = self.st[self.si % len(self.st)]
                self.si += 1
                bfb = self.bf[self.bi % len(self.bf)]
                self.bi += 1
                v = src_ap.rearrange("(k p) n -> p k n", p=128)
                h = max(nk // 2, 1)
                fns = [DMA(st[:, 0:h, 0:width], v[:, 0:h, :])]
                if nk > h:
                    fns.append(DMA(st[:, h:nk, 0:width], v[:, h:nk, :]))
                P.dma(dma_eng, fns, WREADS, [st])
                P.op("pool", CP(bfb[:, 0:nk, 0:width], st[:, 0:nk, 0:width]), [st], [bfb])
                return bfb

        def proj(pb_ap, wb, c0, ncols, tt, buf_pb, n=TT, rhs_fn=None):
            for k in range(8):
                rhs = hT[:, k, tt * TT:tt * TT + n] if rhs_fn is None else rhs_fn(k)
                P.op("pe", MM(pb_ap, wb[:, k, c0:c0 + ncols], rhs, k == 0, k == 7), [wb, hT, hTh], [buf_pb])

        for l in layers:
            xsrc = xT_d if l == 0 else x1_d
            WREADS[:] = [wfull_l[l]]
            xdst = x1_d if l == 0 else yT_d
            Acol = lambda k: der[:, l, k:k + 1]
            Bcol = lambda k: der[:, l, 8 + k:9 + k]

            with ExitStack() as es:
                xts = [P.sb(es, f"xt{i}", [128, 8, TT], F32) for i in range(2)]
                sq = P.sb(es, "sq", [128, 8, TT], F32)
                rstd = P.sb(es, "rstd", [128, TT], F32)
                xh = P.sb(es, "xh", [128, 8, 4], F32)
                xhf = P.sb(es, "xhf", [128, 32], F32)
                xhr = P.sb(es, "xhr", [128, 8, 32], F32)
                for tt in range(NT + 1):
                    halo = tt == NT
                    n = 4 if halo else TT
                    if halo:
                        xt = xh
                        if l == 0:
                            P.dma("sp", DMA(xhf[:], xh0_d[:, :]), [xh0_d], [xh, xhf])
                        else:
                            for j in range(NCORE):
                                P.dma("sp", DMA(xhr[:, j, :], xhrecv[j * 128:(j + 1) * 128, :]), [xhrecv], [xhr])
                            P.op("dve", TS(xhf[:], xhr[:, 0, :], flags[:, 16:17], None, ALU.mult), [xhr, flags], [xhf])
                            for j in range(1, NCORE):
                                P.op("dve", STT(xhf[:], xhr[:, j, :], flags[:, 16 + j:17 + j], xhf[:], ALU.mult, ALU.add), [xhr, flags, xhf], [xhf])
                        P.op("dve", CP(xh[:], xhf[:].rearrange("p (k n) -> p k n", n=4)), [xhf], [xh])
                    else:
                        xt = xts[tt % 2]
                        v = xsrc.h.ap()[:, tt * TT:(tt + 1) * TT].rearrange("(k p) n -> p k n", p=128)
                        P.dma("sp", [DMA(xt[:, 0:4, :], v[:, 0:4, :]), DMA(xt[:, 4:8, :], v[:, 4:8, :])], [xsrc], [xt])
                    P.op("act", ACT(sq[:, :, 0:n], xt[:, :, 0:n], AF.Square), [xt], [sq])
                    pb = bank()
                    for k in range(8):
                        P.op("pe", MM(pb[:, 0:n], mean1024, sq[:, k, 0:n], k == 0, k == 7), [cf, sq], [pb])
                    P.op("act", ACT(rstd[:, 0:n], pb[:, 0:n], AF.Ln, bias=cf[:, 386:387]), [pb, cf], [rstd])
                    P.op("act", ACT(rstd[:, 0:n], rstd[:, 0:n], AF.Exp, scale=-0.5), [rstd], [rstd])
                    P.op("dve", TT_(xt[:, :, 0:n], xt[:, :, 0:n], rstd[:, 0:n].unsqueeze(1).to_broadcast([128, 8, n]), ALU.mult), [xt, rstd], [xt])
                    for k in range(8):
                        dst = hTh[:, k, :] if halo else hT[:, k, tt * TT:(tt + 1) * TT]
                        P.op("act", ACT(dst, xt[:, k, 0:n], AF.Identity, bias=Bcol(k), scale=Acol(k)), [xt, der], [hTh if halo else hT])
                P.barrier(coll=False)
            if debug == "B":
                break

            with ExitStack() as es:
                S = Slabs(es, nst=1, nbf=3)
                cosT = P.sb(es, "cosT", [128, T], F32)
                sinS = P.sb(es, "sinS", [128, T], F32)
                with ExitStack() as es2:
                    pi_ = P.sb(es2, "pos_i", [128, T], I32)
                    pf = P.sb(es2, "pos_f", [128, T], F32)
                    ang = P.sb(es2, "ang", [128, T], F32)
                    kf = P.sb(es2, "kf", [128, T], F32)
                    ki = P.sb(es2, "ki", [128, T], I32)
                    P.dma("sp", DMA(pi_[:], pos_d[:]), [pos_d], [pi_])
                    P.op("dve", CP(pf[:], pi_[:]), [pi_], [pf])
                    C1 = 6.28125
                    C2 = float(2 * np.pi - 6.28125)
                    for which, shift, dst in (("sin", 0.0, sinS), ("cos", float(np.pi / 2), cosT)):
                        P.op("dve", TS(ang[:], pf[:], cf[:, 384:385], shift, ALU.mult, ALU.add), [pf, cf], [ang])
                        P.op("dve", TS(kf[:], ang[:], float(1 / (2 * np.pi)), 0.5, ALU.mult, ALU.add), [ang], [kf])
                        P.op("dve", CP(ki[:], kf[:]), [kf], [ki])
                        P.op("dve", CP(kf[:], ki[:]), [ki], [kf])
                        P.op("dve", STT(ang[:], kf[:], -C1, ang[:], ALU.mult, ALU.add), [kf, ang], [ang])
                        P.op("dve", STT(ang[:], kf[:], -C2, ang[:], ALU.mult, ALU.add), [kf, ang], [ang])
                        P.op("dve", lambda e: e.tensor_single_scalar(kf[:], ang[:], float(-np.pi), ALU.is_lt), [ang], [kf])
                        P.op("dve", STT(ang[:], kf[:], float(2 * np.pi), ang[:], ALU.mult, ALU.add), [kf, ang], [ang])
                        P.op("dve", lambda e: e.tensor_single_scalar(kf[:], ang[:], float(np.pi), ALU.is_gt), [ang], [kf])
                        P.op("dve", STT(ang[:], kf[:], float(-2 * np.pi), ang[:], ALU.mult, ALU.add), [kf, ang], [ang])
                        P.op("act", ACT(dst[:], ang[:], AF.Sin), [ang], [dst])
                    P.op("dve", TS(sinS[:], sinS[:], cf[:, 385:386], None, ALU.mult), [sinS, cf], [sinS])

                    P.barrier(coll=False)
                sqt_r = Ring(es, "sqt", [128, TT], F32, 3)
                rs_r = Ring(es, "rs", [128, TT], F32, 3)
                xn_r = Ring(es, "xn", [128, TT], F32, 3)
                xnb_r = Ring(es, "xnb", [128, TT], BF16, 3)
                t1_r = Ring(es, "t1", [128, TT], F32, 3)
                t2_r = Ring(es, "t2", [128, TT], F32, 3)
                qko = [P.sb(es, f"qko{i}", [128, T], BF16) for i in range(3)]
                vst = P.sb(es, "vst", [128, 4, 16, 128], BF16)
                for g in range(3):
                    d = DILS[g]
                    for which in range(2):
                        wsl = S.load(w_in_d.h.ap()[l, :, (O_BQ if which == 0 else O_BK) + g * 512:(O_BQ if which == 0 else O_BK) + (g + 1) * 512])
                        gcol = der[:, l, 40 + g:41 + g] if which == 0 else vec[:, l, 75 + g:76 + g]
                        for hp in range(4):
                            out = qko[(which * 4 + hp) % 3]
                            for tt in range(NT):
                                sl = slice(tt * TT, (tt + 1) * TT)
                                pb = bank()
                                proj(pb[:, :], wsl, hp * 128, 128, tt, pb)
                                sqt, rs, xn, xnb, t1, t2 = sqt_r.next(), rs_r.next(), xn_r.next(), xnb_r.next(), t1_r.next(), t2_r.next()
                                P.op("act", ACT(sqt[:], pb[:, :], AF.Square), [pb], [sqt])
                                pn = bank()
                                P.op("pe", MM(pn[:, :], mean64, sqt[:], True, True), [cf, sqt], [pn])
                                P.op("act", ACT(rs[:], pn[:, :], AF.Ln, bias=cf[:, 386:387]), [pn, cf], [rs])
                                P.op("act", ACT(rs[:], rs[:], AF.Exp, scale=-0.5), [rs], [rs])
                                P.op("dve", STT(xn[:], pb[:, :], gcol, rs[:], ALU.mult, ALU.mult), [pb, der, vec, rs], [xn])
                                P.op("act", lambda e, xnb=xnb, xn=xn: e.copy(xnb[:], xn[:]), [xn], [xnb])
                                pr = bank()
                                P.op("pe", MM(pr[:, :], rperm, xnb[:], True, True), [cbf, xnb], [pr])
                                P.op("pool", TT_(t1[:], xn[:], cosT[:, sl], ALU.mult), [xn, cosT], [t1])
                                P.op("dve", TT_(t2[:], pr[:, :], sinS[:, sl], ALU.mult), [pr, sinS], [t2])
                                P.op("pool", TT_(out[:, sl], t1[:], t2[:], ALU.add), [t1, t2], [out])
                            dst = q_d if which == 0 else k_d
                            P.dma("sp", DMA(dst[g * 4 + hp], out[:]), [out], [dst])
                            if which == 1:
                                o = kv_off(g, hp, 0)
                                P.dma("sp", DMA(kvsend[:, o:o + 128 * d], out[:, T - 128 * d:T]), [out], [kvsend])
                    wvs = S.load(w_in_d.h.ap()[l, :, O_BV + g * 512:O_BV + (g + 1) * 512])
                    for blk in range(16):
                        n_, r_ = blk // d, blk % d
                        t0 = n_ * 128 * d + r_
                        pb = bank()
                        for k in range(8):
                            P.op("pe", MM(pb[:, :], hT[:, k, t0:t0 + 127 * d + 1:d], wvs[:, k, :], k == 0, k == 7), [hT, wvs], [pb])
                        P.op("act", lambda e, pb=pb, blk=blk: e.copy(vst[:, :, blk, :], pb[:, :].rearrange("p (a b) -> p a b", b=128)), [pb], [vst])
                    for hp in range(4):
                        P.dma("sp", DMA(v_d[g * 4 + hp].rearrange("p (a b) -> p a b", b=128), vst[:, hp, :, :]), [vst], [v_d])
                        o = kv_off(g, hp, 1)
                        P.dma("sp", DMA(kvsend[:, o:o + 128 * d].rearrange("p (a b) -> p a b", b=128), vst[:, hp, 16 - d:16, :]), [vst], [kvsend])
                P.barrier(coll=False)

            P.coll(lambda e: e.collective_compute("AllGather", ALU.bypass, replica_groups=[list(range(NCORE))],
                                                         ins=[kvsend.h.ap().opt()], outs=[kvrecv.h.ap().opt()]), [kvsend], [kvrecv], key="kv")

            with ExitStack() as es:
                S = Slabs(es, nst=1, nbf=2)
                wst = P.sb(es, "gw_st", [128, 2, 8, 128], F32)
                wbd = P.sb(es, "gw_bd", [128, 2, 8, 128], BF16)
                xa_r = Ring(es, "xa", [128, 3 + T], F32, 2)
                xc_r = Ring(es, "xc", [128, T], F32, 2)
                xcb_r = Ring(es, "xcb", [128, T], BF16, 2)
                rt_r = Ring(es, "rt", [128, TT], F32, 3)
                it_r = Ring(es, "it", [128, TT], F32, 3)
                a2t_r = Ring(es, "a2t", [128, TT], F32, 3)
                af = P.sb(es, "af", [128, T], F32)
                uf = P.sb(es, "uf", [128, T], F32)
                hl = P.sb(es, "hl", [128, T], F32)
                At = P.sb(es, "At", [128, T], F32)
                sg = P.sb(es, "sg", [128, T], F32)
                yo = P.sb(es, "yo", [128, T], BF16)
                ao = P.sb(es, "ao", [128, T], BF16)
                fin = P.sb(es, "fin", [128, 20], F32)
                P.op("pool", lambda e: e.memset(wst[:], 0.0), [], [wst])
                for wi, wd in enumerate((lru_wa_d, lru_wx_d)):
                    fns = []
                    for half in range(2):
                        src = wd.h.ap()[l].rearrange("(c t) j k -> t j c k", t=2)[half]
                        fns.append(DMA(wst[half * 64:(half + 1) * 64, wi, :, half * 64:(half + 1) * 64], src))
                    P.dma("sp", fns, [wd], [wst])
                P.op("pool", CP(wbd[:], wst[:]), [wst], [wbd])
                for s4 in range(2):
                    wax = S.load(w_in_d.h.ap()[l, :, O_AX + s4 * 512:O_AX + (s4 + 1) * 512])
                    wag = S.load(w_in_d.h.ap()[l, :, O_AG + s4 * 512:O_AG + (s4 + 1) * 512])
                    for c4 in range(4):
                        cc = s4 * 4 + c4
                        xa, xc, xcb = xa_r.next(), xc_r.next(), xcb_r.next()
                        col = lambda base: vec[:, l, base + cc:base + cc + 1]
                        dcol = lambda base: der[:, l, base + cc:base + cc + 1]
                        for tt in range(NT):
                            pb = bank()
                            proj(pb[:, :], wax, c4 * 128, 128, tt, pb)
                            P.op("act", lambda e, pb=pb, tt=tt, xa=xa: e.copy(xa[:, 3 + tt * TT:3 + (tt + 1) * TT], pb[:, :]), [pb], [xa])
                        pb = bank()
                        proj(pb[:, 0:4], wax, c4 * 128, 128, 0, pb, n=4, rhs_fn=lambda k: hTh[:, k, :])
                        P.op("dve", TS(xa[:, 0:3], pb[:, 0:3], flags[:, 0:1], None, ALU.mult), [pb, flags], [xa])
                        cw = lambda k: vec[:, l, 16 + cc * 4 + k:17 + cc * 4 + k]
                        P.op("dve", TS(xc[:], xa[:, 0:T], cw(0), col(8), ALU.mult, ALU.add), [xa, vec], [xc])
                        for k in range(1, 4):
                            P.op("dve", STT(xc[:], xa[:, k:k + T], cw(k), xc[:], ALU.mult, ALU.add), [xa, vec, xc], [xc])
                        P.op("act", lambda e, xcb=xcb, xc=xc: e.copy(xcb[:], xc[:]), [xc], [xcb])
                        for tt in range(NT):
                            sl = slice(tt * TT, (tt + 1) * TT)
                            rt, it, a2t = rt_r.next(), it_r.next(), a2t_r.next()
                            pr = bank()
                            P.op("pe", MM(pr[:, :], wbd[:, 0, cc, :], xcb[:, sl], True, True), [wbd, xcb], [pr])
                            pi2 = bank()
                            P.op("pe", MM(pi2[:, :], wbd[:, 1, cc, :], xcb[:, sl], True, True), [wbd, xcb], [pi2])
                            P.op("act", ACT(rt[:], pr[:, :], AF.Sigmoid, bias=col(48)), [pr, vec], [rt])
                            P.op("act", ACT(it[:], pi2[:, :], AF.Sigmoid, bias=col(56)), [pi2, vec], [it])
                            P.op("act", ACT(af[:, sl], rt[:], AF.Exp, scale=dcol(24)), [rt, der], [af])
                            P.op("dve", TS(sg[:, sl], rt[:], dcol(24), None, ALU.mult), [rt, der], [sg])
                            P.op("act", ACT(a2t[:], rt[:], AF.Exp, scale=dcol(32)), [rt, der], [a2t])
                            P.op("act", ACT(a2t[:], a2t[:], AF.Sqrt, bias=1.0, scale=-1.0), [a2t], [a2t])
                            P.op("dve", TT_(it[:], it[:], xc[:, sl], ALU.mult), [it, xc], [it])
                            P.op("dve", TT_(uf[:, sl], it[:], a2t[:], ALU.mult), [it, a2t], [uf])
                        P.op("dve", lambda e: e.tensor_tensor_scan(hl[:], af[:], uf[:], 0.0, ALU.mult, ALU.add), [af, uf], [hl])
                        P.op("dve", lambda e: e.tensor_tensor_scan(At[:], onesT, sg[:], 0.0, ALU.mult, ALU.add), [cbf, sg], [At])
                        P.op("act", ACT(At[:], At[:], AF.Exp), [At], [At])
                        P.op("dve", CP(fin[:, cc:cc + 1], hl[:, T - 1:T]), [hl], [fin])
                        P.op("dve", CP(fin[:, 8 + cc:9 + cc], At[:, T - 1:T]), [At], [fin])
                        for tt in range(NT):
                            pb = bank()
                            proj(pb[:, :], wag, c4 * 128, 128, tt, pb)
                            P.op("act", ACT(sg[:, tt * TT:(tt + 1) * TT], pb[:, :], AF.Silu), [pb], [sg])
                        P.op("dve", TT_(yo[:], hl[:], sg[:], ALU.mult), [hl, sg], [yo])
                        P.op("pool", TT_(ao[:], At[:], sg[:], ALU.mult), [At, sg], [ao])
                        P.dma("sp", DMA(ya_d[cc * 128:(cc + 1) * 128, :], yo[:]), [yo], [ya_d])
                        P.dma("sp", DMA(ag_d[cc * 128:(cc + 1) * 128, :], ao[:]), [ao], [ag_d])
                P.dma("sp", DMA(finsend[:, 0:16], fin[:, 0:16]), [fin], [finsend])
                P.barrier(coll=False)

            with ExitStack() as es:
                S = Slabs(es, nst=1, nbf=2, tag="qk")
                Sv = Slabs(es, nbf=1, tag="v", st=S.st)
                a1s = P.sb(es, "a1s", [128, 8, 16], F32)
                a1b = P.sb(es, "a1b", [128, 8, 16], BF16)
                a2s = P.sb(es, "a2s", [16, 512], F32)
                a2b = P.sb(es, "a2b", [16, 512], BF16)
                la1b = P.sb(es, "la1b", [16, T], BF16)
                vtok = P.sb(es, "vtok", [128, 16, 256], BF16)
                la = P.sb(es, "la", [128, T], F32)
                eb = P.sb(es, "eb", [128, T], F32)
                eB = P.sb(es, "eB", [128, T], F32)
                kinvf = P.sb(es, "kinvf", [128, T], F32)
                decay = P.sb(es, "decay", [128, 32], F32)
                qdec = P.sb(es, "qdec", [128, T], BF16)
                qglob = P.sb(es, "qglob", [128, T], BF16)
                kinv = P.sb(es, "kinv", [128, T], BF16)
                kend = P.sb(es, "kend", [128, T], BF16)
                kendT = P.sb(es, "kendT", [128, 16, 128], BF16)
                Sst = P.sb(es, "Sst", [128, 256], F32)
                Sbs = [P.sb(es, f"Sb{i}", [128, 256], BF16) for i in range(4)]
                att = [P.sb(es, f"att{i}", [128, 128], BF16) for i in range(2)]
                ost = [P.sb(es, f"ost{i}", [128, 2, 128], F32) for i in range(3)]
                dt4 = P.sb(es, "dt4", [128, 16], F32)
                P.op("pool", lambda e: e.memset(dt4[:], 0.0), [], [dt4])
                P.dma("sp", DMA(a1s[:], a1_d.h.ap()[l].rearrange("(k p) n -> p k n", p=128)), [a1_d], [a1s])
                P.dma("sp", DMA(a2s[:], a2_d.h.ap()[l]), [a2_d], [a2s])
                P.op("pool", CP(a1b[:], a1s[:]), [a1s], [a1b])
                P.op("pool", CP(a2b[:], a2s[:]), [a2s], [a2b])
                for tt in range(NT):
                    pb = bank()
                    for k in range(8):
                        P.op("pe", MM(pb[0:16, :], a1b[:, k, :], hT[:, k, tt * TT:(tt + 1) * TT], k == 0, k == 7), [a1b, hT], [pb])
                    P.op("act", lambda e, pb=pb, tt=tt: e.copy(la1b[:, tt * TT:(tt + 1) * TT], pb[0:16, :]), [pb], [la1b])
                wq = S.load(w_in_d.h.ap()[l, :, O_CQ:O_CQ + 512])
                wk = S.load(w_in_d.h.ap()[l, :, O_CK:O_CK + 512])
                osi = 0
                for hh in range(4):
                    wvs = Sv.load(w_in_d.h.ap()[l, :, O_CV + hh * 256:O_CV + (hh + 1) * 256], width=256)
                    for p in range(16):
                        pb = bank()
                        for k in range(8):
                            P.op("pe", MM(pb[:, 0:256], hT[:, k, p * 128:(p + 1) * 128], wvs[:, k, 0:256], k == 0, k == 7), [hT, wvs], [pb])
                        if p % 2 == 0:
                            P.op("act", lambda e, pb=pb, p=p: e.copy(vtok[:, p, :], pb[:, 0:256]), [pb], [vtok])
                        else:
                            P.op("dve", CP(vtok[:, p, :], pb[:, 0:256]), [pb], [vtok])
                    for tt in range(NT):
                        sl = slice(tt * TT, (tt + 1) * TT)
                        pb = bank()
                        P.op("pe", MM(pb[:, :], a2b[:, hh * 128:(hh + 1) * 128], la1b[:, sl], True, True), [a2b, la1b], [pb])
                        P.op("act", ACT(la[:, sl], pb[:, :], AF.Exp, bias=der[:, l, 44 + hh:45 + hh], scale=-1.0), [pb, der], [la])
                    P.op("act", ACT(la[:], la[:], AF.Ln, bias=1.0), [la], [la])
                    P.op("dve", TS(la[:], la[:], -1.0 / 16.0, None, ALU.mult), [la], [la])
                    P.op("dve", lambda e: e.tensor_tensor_scan(eb[:], resetm, la[:], 0.0, ALU.mult, ALU.add), [cbf, la], [eb])
                    P.op("dve", lambda e: e.tensor_tensor_scan(eB[:], onesT, la[:], 0.0, ALU.mult, ALU.add), [cbf, la], [eB])
                    P.op("act", ACT(kinvf[:], eb[:], AF.Exp, scale=-1.0), [eb], [kinvf])
                    P.op("act", ACT(eb[:], eb[:], AF.Exp), [eb], [eb])
                    P.op("act", ACT(eB[:], eB[:], AF.Exp), [eB], [eB])
                    P.op("pool", CP(decay[:], eb[:].rearrange("p (c j) -> p c j", j=64)[:, :, 63]), [eb], [decay])
                    P.op("dve", CP(dt4[:, hh:hh + 1], eB[:, T - 1:T]), [eB], [dt4])
                    for tt in range(NT):
                        sl = slice(tt * TT, (tt + 1) * TT)
                        pb = bank()
                        proj(pb[:, :], wq, hh * 128, 128, tt, pb)
                        P.op("dve", STT(qdec[:, sl], pb[:, :], float(128 ** -0.5), eb[:, sl], ALU.mult, ALU.mult), [pb, eb], [qdec])
                        P.op("dve", STT(qglob[:, sl], pb[:, :], float(128 ** -0.5), eB[:, sl], ALU.mult, ALU.mult), [pb, eB], [qglob])
                        pb = bank()
                        proj(pb[:, :], wk, hh * 128, 128, tt, pb)
                        P.op("dve", TT_(kinvf[:, sl], pb[:, :], kinvf[:, sl], ALU.mult), [pb, kinvf], [kinvf])
                    P.dma("sp", DMA(qg_d[hh * 128:(hh + 1) * 128, :], qglob[:]), [qglob], [qg_d])
                    P.op("act", lambda e: e.copy(kinv[:], kinvf[:]), [kinvf], [kinv])
                    P.op("pool", TT_(kend[:].rearrange("p (c j) -> p c j", j=64), kinvf[:].rearrange("p (c j) -> p c j", j=64),
                                     decay[:].unsqueeze(2).to_broadcast([128, 32, 64]), ALU.mult), [kinvf, decay], [kend])
                    for p4 in range(4):
                        for pp in range(4):
                            p = p4 * 4 + pp
                            P.op("pe", lambda e, p=p, pp=pp: e.transpose(PSH[:, pp * 128:(pp + 1) * 128], kend[:, p * 128:(p + 1) * 128], ident), [kend, cbf], [PSH])
                        P.op("act", lambda e, p4=p4: e.copy(kendT[:, p4 * 4:(p4 + 1) * 4, :], PSH[:, 0:512].rearrange("p (a b) -> p a b", b=128)), [PSH], [kendT])
                    P.op("pool", lambda e: e.memset(Sst[:], 0.0), [], [Sst])
                    P.op("pool", lambda e: e.memset(Sbs[3][:], 0.0), [], [Sbs[3]])
                    sbi = 3
                    for p in range(16):
                        ts_ = slice(p * 128, (p + 1) * 128)
                        pa = bank()
                        P.op("pe", MM(pa[:, 0:128], kinv[:, ts_], qdec[:, ts_], True, True), [kinv, qdec], [pa])
                        at = att[p % 2]
                        P.op("dve", TT_(at[:], pa[:, 0:128], maskpair, ALU.mult), [pa, cbf], [at])
                        pos_ = [bank(), bank()]
                        sb_prev = Sbs[sbi % 4]
                        for half in range(2):
                            po = pos_[half]
                            P.op("pe", MM(po[:, 0:128], vtok[:, p, half * 128:(half + 1) * 128], at[:], True, False), [vtok, at], [po])
                            P.op("pe", MM(po[:, 0:64], sb_prev[:, half * 128:(half + 1) * 128], qdec[:, p * 128:p * 128 + 64], False, False), [sb_prev, qdec], [po])
                        for c2 in range(2):
                            pkv = bank()
                            P.op("pe", MM(pkv[:, 0:256], kendT[c2 * 64:(c2 + 1) * 64, p, :], vtok[c2 * 64:(c2 + 1) * 64, p, :], True, True), [kendT, vtok], [pkv])
                            P.op("dve", STT(Sst[:], Sst[:], decay[:, 2 * p + c2:2 * p + c2 + 1], pkv[:, 0:256], ALU.mult, ALU.add), [Sst, decay, pkv], [Sst])
                            sbi += 1
                            sb_new = Sbs[sbi % 4]
                            P.op("act", lambda e, sb_new=sb_new: e.copy(sb_new[:], Sst[:]), [Sst], [sb_new])
                            if c2 == 0:
                                for half in range(2):
                                    po = pos_[half]
                                    P.op("pe", MM(po[:, 64:128], sb_new[:, half * 128:(half + 1) * 128], qdec[:, p * 128 + 64:(p + 1) * 128], False, True), [sb_new, qdec], [po])
                        o_ = ost[osi % 3]
                        osi += 1
                        for half in range(2):
                            P.op("act", lambda e, po=pos_[half], o_=o_, half=half: e.copy(o_[:, half, :], po[:, 0:128]), [pos_[half]], [o_])
                        P.dma("sp", DMA(ol_d.h.ap()[hh * 256:(hh + 1) * 256, p * 128:(p + 1) * 128].rearrange("(h p) q -> p h q", p=128), o_[:]), [o_], [ol_d])
                    P.dma("sp", DMA(finsend[:, 32 + hh * 256:32 + (hh + 1) * 256], Sst[:]), [Sst], [finsend])
                P.dma("sp", DMA(finsend[:, 16:32], dt4[:]), [dt4], [finsend])
                P.barrier(coll=False)

            P.coll(lambda e: e.collective_compute("AllGather", ALU.bypass, replica_groups=[list(range(NCORE))],
                                                         ins=[finsend.h.ap().opt()], outs=[finrecv.h.ap().opt()]), [finsend], [finrecv], key="fin")

            with ExitStack() as es:
                ohI = P.sb(es, "ohI", [128, 8, 128], BF16)
                slots = [P.sb(es, f"slot{i}", [128, 7, 2048], BF16) for i in range(2)]
                selo = [P.sb(es, f"selo{i}", [128, 2048], BF16) for i in range(2)]
                for j in range(7):
                    P.op("dve", TS(ohI[:, j, :], ident, flags[:, 16 + j:17 + j], None, ALU.mult), [cbf, flags], [ohI])
                nit = (KVW + 2047) // 2048
                for it in range(nit):
                    c0 = it * 2048
                    w = min(2048, KVW - c0)
                    sl_ = slots[it % 2]
                    so = selo[it % 2]
                    P.dma("sp", DMA(sl_[:, :, 0:w], kvrecv.h.ap()[0:7 * 128, c0:c0 + w].rearrange("(r p) n -> p r n", p=128)), [kvrecv], [sl_])
                    for q in range(w // 512):
                        pb = bank()
                        for j in range(7):
                            P.op("pe", MM(pb[:, :], ohI[:, j, :], sl_[:, j, q * 512:(q + 1) * 512], j == 0, j == 6), [ohI, sl_], [pb])
                        P.op("act", lambda e, so=so, pb=pb, q=q: e.copy(so[:, q * 512:(q + 1) * 512], pb[:, :]), [pb], [so])
                    P.dma("sp", DMA(kvprev[:, c0:c0 + w], so[:, 0:w]), [so], [kvprev])
            P.barrier(coll=False)

            def merge_branch(es, S, kb, yT_, nk, projd, tag):
                stg = [P.sb(es, f"stg{tag}{i}", [128, TT], F32) for i in range(3)]
                sgm = [P.sb(es, f"sgm{tag}{i}", [128, TT], F32) for i in range(2)]
                cnt = 0
                for och in range(2):
                    pw = S.load(projd.h.ap()[l, :, och * 512:(och + 1) * 512], nk=nk)
                    mgw = S.load(w_in_d.h.ap()[l, :, O_MG + kb * 1024 + och * 512:O_MG + kb * 1024 + (och + 1) * 512])
                    for oc4 in range(4):
                        oc = och * 4 + oc4
                        for tt in range(NT):
                            sl = slice(tt * TT, (tt + 1) * TT)
                            pp = bank()
                            for kc in range(nk):
                                P.op("pe", MM(pp[:, :], pw[:, kc, oc4 * 128:(oc4 + 1) * 128], yT_[:, kc, sl], kc == 0, kc == nk - 1), [pw, yT_], [pp])
                            pg = bank()
                            proj(pg[:, :], mgw, oc4 * 128, 128, tt, pg)
                            sm = sgm[cnt % 2]
                            st = stg[cnt % 3]
                            cnt += 1
                            P.op("act", ACT(sm[:], pg[:, :], AF.Sigmoid), [pg], [sm])
                            P.op("dve", TT_(st[:], pp[:, :], sm[:], ALU.mult), [pp, sm], [st])
                            P.dma("sp", DMA(tk_d[kb][oc * 128:(oc + 1) * 128, sl], st[:]), [st], [tk_d[kb]])

            with ExitStack() as es:
                S = Slabs(es, nst=1, nbf=3)
                qt = [P.sb(es, f"qt{i}", [128, T], BF16) for i in range(1)]
                kt = [P.sb(es, f"kt{i}", [128, 2 * T], BF16) for i in range(1)]
                vt = [P.sb(es, f"vt{i}", [128, 32, 128], BF16) for i in range(1)]
                acc = P.sb(es, "acc", [128, 2, T], F32)
                pT = [P.sb(es, f"pT{i}", [128, 256], BF16) for i in range(3)]
                rec = P.sb(es, "rec", [128, T], F32)
                sgb = P.sb(es, "sgb", [128, TT], F32)
                ybT = P.sb(es, "ybT", [128, 4, T], BF16)
                wbg = S.load(w_in_d.h.ap()[l, :, O_BG:O_BG + 512])
                ci = 0
                pti = 0
                for hp in range(4):
                    P.op("pool", lambda e: e.memset(acc[:], 0.0), [], [acc])
                    for g in range(3):
                        d = DILS[g]
                        q_, k_, v_ = qt[0], kt[0], vt[0]
                        ci += 1
                        P.dma("sp", DMA(q_[:], q_d[g * 4 + hp]), [q_d], [q_])
                        ok_, ov_ = kv_off(g, hp, 0), kv_off(g, hp, 1)
                        P.dma("sp", [DMA(k_[:, T:2 * T], k_d[g * 4 + hp]), DMA(k_[:, T - 128 * d:T], kvprev[:, ok_:ok_ + 128 * d])],
                              [k_d, kvprev], [k_])
                        P.dma("act", [DMA(v_[:, 16:32, :], v_d[g * 4 + hp].rearrange("p (a b) -> p a b", b=128)),
                                      DMA(v_[:, 16 - d:16, :].rearrange("p a b -> p (a b)"), kvprev[:, ov_:ov_ + 128 * d])],
                              [v_d, kvprev], [v_])
                        for blk in range(16):
                            n_, r_ = blk // d, blk % d
                            t0 = n_ * 128 * d + r_
                            qs = lambda hb: q_[hb:hb + 64, t0:t0 + 127 * d + 1:d]
                            kcur = lambda hb: k_[hb:hb + 64, T + t0:T + t0 + 127 * d + 1:d]
                            kprev = lambda hb: k_[hb:hb + 64, T + t0 - 128 * d:T + t0 - d + 1:d]
                            nm = negm[:, 256:512] if n_ == 0 else negm[:, 0:256]
                            for h in range(2):
                                hb = 64 * h
                                psS = bank()
                                P.op("pe", MM(psS[:, 0:256], ident, nm, True, False), [cbf, negm], [psS])
                                P.op("pe", MM(psS[:, 0:128], kprev(hb), qs(hb), False, False), [k_, q_], [psS])
                                P.op("pe", MM(psS[:, 128:256], kcur(hb), qs(hb), False, True), [k_, q_], [psS])
                                p_ = pT[pti % 3]
                                pti += 1
                                P.op("act", ACT(p_[:], psS[:, 0:256], AF.Exp), [psS], [p_])
                                psO = bank()
                                P.op("pe", MM(psO[:, 0:128], v_[:, 16 + blk - d, :], p_[:, 0:128], True, False), [v_, p_], [psO])
                                P.op("pe", MM(psO[:, 0:128], v_[:, 16 + blk, :], p_[:, 128:256], False, True), [v_, p_], [psO])
                                P.op("pe", MM(psO[:, 128:256], onesb, p_[:, 0:128], True, False), [cbf, p_], [psO])
                                P.op("pe", MM(psO[:, 128:256], onesb, p_[:, 128:256], False, True), [cbf, p_], [psO])
                                av = acc[hb:hb + 64, :, t0:t0 + 127 * d + 1:d]
                                P.op("dve", TT_(av, av, psO[hb:hb + 64, 0:256].rearrange("p (a b) -> p a b", b=128), ALU.add), [acc, psO], [acc])
                    P.op("dve", lambda e: e.reciprocal(rec[:], acc[:, 1, :]), [acc], [rec])
                    P.op("dve", TT_(rec[:], rec[:], acc[:, 0, :], ALU.mult), [rec, acc], [rec])
                    for tt in range(NT):
                        sl = slice(tt * TT, (tt + 1) * TT)
                        pg = bank()
                        proj(pg[:, :], wbg, hp * 128, 128, tt, pg)
                        P.op("act", ACT(sgb[:], pg[:, :], AF.Silu), [pg], [sgb])
                        P.op("dve", TT_(ybT[:, hp, sl], rec[:, sl], sgb[:], ALU.mult), [rec, sgb], [ybT])
                if debug and l == 0:
                    for k in range(4):
                        P.dma("sp", DMA(dbg["ybT"][k * 128:(k + 1) * 128, :], ybT[:, k, :]), [ybT], [dbg["ybT"]])
                merge_branch(es, S, 1, ybT, 4, proj_b_d, "b")
                P.barrier(coll=False)

            with ExitStack() as es:
                S = Slabs(es, nst=1, nbf=3)
                finr = P.sb(es, "finr", [128, 8, 32], F32)
                hin = P.sb(es, "hin", [128, 8], F32)
                tA = P.sb(es, "tA", [128, 8], F32)
                yaT = P.sb(es, "yaT", [128, 8, T], BF16)
                yl = [P.sb(es, f"yl{i}", [128, T], BF16) for i in range(2)]
                al = [P.sb(es, f"al{i}", [128, T], BF16) for i in range(2)]
                for j in range(NCORE):
                    P.dma("sp", DMA(finr[:, j, :], finrecv[j * 128:(j + 1) * 128, 0:32]), [finrecv], [finr])
                P.op("pool", lambda e: e.memset(hin[:], 0.0), [], [hin])
                for j in range(NCORE):
                    selj = flags[:, 8 + j:9 + j]
                    P.op("dve", TS(tA[:], finr[:, j, 8:16], -1.0, selj, ALU.add, ALU.mult), [finr, flags], [tA])
                    P.op("dve", TS(tA[:], tA[:], 1.0, None, ALU.add), [tA], [tA])
                    P.op("dve", TT_(hin[:], hin[:], tA[:], ALU.mult), [hin, tA], [hin])
                    P.op("dve", STT(hin[:], finr[:, j, 0:8], selj, hin[:], ALU.mult, ALU.add), [finr, flags, hin], [hin])
                if debug and l == 0:
                    P.dma("sp", DMA(dbg["hin"][:, :], hin[:]), [hin], [dbg["hin"]])
                    P.dma("sp", DMA(dbg["finr"][:, :], finr[:].rearrange("p r n -> p (r n)")), [finr], [dbg["finr"]])
                for cc in range(8):
                    y_, a_ = yl[cc % 2], al[cc % 2]
                    P.dma("sp", DMA(y_[:], ya_d[cc * 128:(cc + 1) * 128, :]), [ya_d], [y_])
                    P.dma("sp", DMA(a_[:], ag_d[cc * 128:(cc + 1) * 128, :]), [ag_d], [a_])
                    P.op("dve", STT(yaT[:, cc, :], a_[:], hin[:, cc:cc + 1], y_[:], ALU.mult, ALU.add), [a_, hin, y_], [yaT])
                if debug and l == 0:
                    for k in range(8):
                        P.dma("sp", DMA(dbg["yaT"][k * 128:(k + 1) * 128, :], yaT[:, k, :]), [yaT], [dbg["yaT"]])
                merge_branch(es, S, 0, yaT, 8, proj_a_d, "a")
                P.barrier()

            with ExitStack() as es:
                S = Slabs(es, nst=1, nbf=4)
                finr = P.sb(es, "finrc", [128, 8, 32], F32)
                tD = P.sb(es, "tD", [128, 8, 4], F32)
                Sin = P.sb(es, "Sin", [128, 256], F32)
                Sj = [P.sb(es, f"Sj{i}", [128, 256], F32) for i in range(2)]
                Sib = P.sb(es, "Sib", [128, 256], BF16)
                qg = P.sb(es, "qg", [128, T], BF16)
                ol = P.sb(es, "olc", [128, 2, T], F32)
                sq2_r = Ring(es, "sq2", [128, 2, TT], F32, 2)
                rs2_r = Ring(es, "rs2", [128, TT], F32, 2)
                sgc_r = Ring(es, "sgc", [128, TT], F32, 3)
                yn_r = Ring(es, "yn", [128, TT], F32, 3)
                ycT = P.sb(es, "ycT", [128, 8, T], BF16)
                for j in range(NCORE):
                    P.dma("sp", DMA(finr[:, j, :], finrecv[j * 128:(j + 1) * 128, 0:32]), [finrecv], [finr])
                for j in range(NCORE):
                    selj = flags[:, 8 + j:9 + j]
                    P.op("dve", TS(tD[:, j, :], finr[:, j, 16:20], -1.0, selj, ALU.add, ALU.mult), [finr, flags], [tD])
                    P.op("dve", TS(tD[:, j, :], tD[:, j, :], 1.0, None, ALU.add), [tD], [tD])
                wcg = [S.load(w_in_d.h.ap()[l, :, O_CG + i * 512:O_CG + (i + 1) * 512]) for i in range(2)]
                for hh in range(4):
                    P.op("pool", lambda e: e.memset(Sin[:], 0.0), [], [Sin])
                    for j in range(NCORE):
                        sj = Sj[j % 2]
                        P.dma("sp", DMA(sj[:], finrecv[j * 128:(j + 1) * 128, 32 + hh * 256:32 + (hh + 1) * 256]), [finrecv], [sj])
                        P.op("dve", TS(Sin[:], Sin[:], tD[:, j, hh:hh + 1], None, ALU.mult), [Sin, tD], [Sin])
                        P.op("dve", STT(Sin[:], sj[:], flags[:, 8 + j:9 + j], Sin[:], ALU.mult, ALU.add), [sj, flags, Sin], [Sin])
                    P.op("act", lambda e: e.copy(Sib[:], Sin[:]), [Sin], [Sib])
                    P.dma("sp", DMA(qg[:], qg_d[hh * 128:(hh + 1) * 128, :]), [qg_d], [qg])
                    for half in range(2):
                        P.dma("sp", DMA(ol[:, half, :], ol_d[(hh * 2 + half) * 128:(hh * 2 + half + 1) * 128, :]), [ol_d], [ol])
                    for tt in range(NT):
                        sl = slice(tt * TT, (tt + 1) * TT)
                        for half in range(2):
                            pc = bank()
                            P.op("pe", MM(pc[:, :], Sib[:, half * 128:(half + 1) * 128], qg[:, sl], True, True), [Sib, qg], [pc])
                            P.op("dve", TT_(ol[:, half, sl], ol[:, half, sl], pc[:, :], ALU.add), [ol, pc], [ol])
                        sq2, rs2 = sq2_r.next(), rs2_r.next()
                        P.op("act", ACT(sq2[:], ol[:, :, sl], AF.Square), [ol], [sq2])
                        pn = bank()
                        for half in range(2):
                            P.op("pe", MM(pn[:, :], mean256, sq2[:, half, :], half == 0, half == 1), [cf, sq2], [pn])
                        P.op("act", ACT(rs2[:], pn[:, :], AF.Ln, bias=cf[:, 386:387]), [pn, cf], [rs2])
                        P.op("act", ACT(rs2[:], rs2[:], AF.Exp, scale=-0.5), [rs2], [rs2])
                        for half in range(2):
                            ch = hh * 2 + half
                            pg = bank()
                            sgc, yn = sgc_r.next(), yn_r.next()
                            proj(pg[:, :], wcg[ch // 4], (ch % 4) * 128, 128, tt, pg)
                            P.op("act", ACT(sgc[:], pg[:, :], AF.Silu), [pg], [sgc])
                            P.op("dve", STT(yn[:], ol[:, half, sl], vec[:, l, 82 + half:83 + half], rs2[:], ALU.mult, ALU.mult), [ol, vec, rs2], [yn])
                            P.op("pool", TT_(ycT[:, ch, sl], yn[:], sgc[:], ALU.mult), [yn, sgc], [ycT])
                if debug and l == 0:
                    for k in range(8):
                        P.dma("sp", DMA(dbg["ycT"][k * 128:(k + 1) * 128, :], ycT[:, k, :]), [ycT], [dbg["ycT"]])
                merge_branch(es, S, 2, ycT, 8, proj_c_d, "c")
                P.barrier()

            with ExitStack() as es:
                S = Slabs(es, nst=1, nbf=2)
                wo = [S.load(w_o_d.h.ap()[l, :, i * 512:(i + 1) * 512]) for i in range(2)]
                tks = [[P.sb(es, f"tk{i}{j}", [128, TT], F32) for j in range(3)] for i in range(2)]
                mb = [P.sb(es, f"mb{i}", [128, 8, TT], BF16) for i in range(2)]
                xr = [P.sb(es, f"xr{i}", [128, TT], F32) for i in range(3)]
                xhs = P.sb(es, "xhs", [128, 8, 4], F32)
                if l == 0:
                    P.op("pool", lambda e: e.memset(xhs[:], 0.0), [], [xhs])
                ti = 0
                xi = 0
                for tt in range(NT):
                    sl = slice(tt * TT, (tt + 1) * TT)
                    m_ = mb[tt % 2]
                    for kc in range(8):
                        t3 = tks[ti % 2]
                        ti += 1
                        for b3 in range(3):
                            P.dma("sp" if b3 < 2 else "act", DMA(t3[b3][:], tk_d[b3][kc * 128:(kc + 1) * 128, sl]), [tk_d[b3]], [t3[b3]])
                        P.op("pool", TT_(t3[0][:], t3[0][:], t3[1][:], ALU.add), [t3[0], t3[1]], [t3[0]])
                        P.op("pool", TT_(m_[:, kc, :], t3[0][:], t3[2][:], ALU.add), [t3[0], t3[2]], [m_])
                    for oc in range(8):
                        pb = bank()
                        for kc in range(8):
                            P.op("pe", MM(pb[:, :], wo[oc // 4][:, kc, (oc % 4) * 128:(oc % 4 + 1) * 128], m_[:, kc, :], kc == 0, kc == 7), [wo[oc // 4], m_], [pb])
                        x_ = xr[xi % 3]
                        xi += 1
                        P.dma("act", DMA(x_[:], xsrc[oc * 128:(oc + 1) * 128, sl]), [xsrc], [x_])
                        P.op("dve", STT(x_[:], pb[:, :], der[:, l, 16 + oc:17 + oc], x_[:], ALU.mult, ALU.add), [pb, der, x_], [x_])
                        P.dma("sp", DMA(xdst[oc * 128:(oc + 1) * 128, sl], x_[:]), [x_], [xdst])
                        if l == 0 and tt == NT - 1:
                            P.op("dve", CP(xhs[:, oc, 0:3], x_[:, TT - 3:TT]), [x_], [xhs])
                if l == 0:
                    P.dma("sp", DMA(xhsend.h.ap().rearrange("p (k n) -> p k n", n=4), xhs[:]), [xhs], [xhsend])
                P.barrier()
            if l == 0:
                P.coll(lambda e: e.collective_compute("AllGather", ALU.bypass, replica_groups=[list(range(NCORE))],
                                                             ins=[xhsend.h.ap().opt()], outs=[xhrecv.h.ap().opt()]), [xhsend], [xhrecv], key="xh")
                P.barrier()
            if debug == "L0":
                break
        P.barrier()
    P.emit()
    return nc


def _consts():
    bf = ml_dtypes.bfloat16
    cbf = np.zeros((128, 640 + 2 * T), np.float32)
    cbf[:, 0:128] = np.eye(128)
    m = np.arange(128)
    partner = np.where(m % 64 < 32, m + 32, m - 32)
    rp = np.zeros((128, 128), np.float32)
    rp[partner, m] = 1.0
    cbf[:, 128:256] = rp
    s = np.arange(128)[:, None]
    q = np.arange(128)[None, :]
    cbf[:, 256:384] = ((s // 64 == q // 64) & (s % 64 <= q % 64)).astype(np.float32)
    cbf[:, 384:640] = 1.0
    rm = np.ones(T, np.float32)
    rm[::64] = 0.0
    cbf[:, 640:640 + T] = rm[None, :]
    cbf[:, 640 + T:] = 1.0
    cf = np.zeros((128, 388), np.float32)
    cf[:, 0:128] = 1.0 / 1024
    blk = np.zeros((128, 128), np.float32)
    blk[:64, :64] = 1.0 / 64
    blk[64:, 64:] = 1.0 / 64
    cf[:, 128:256] = blk
    cf[:, 256:384] = 1.0 / 256
    half = 32
    inv = (np.float32(10000.0) ** (-(np.arange(half, dtype=np.float32) / np.float32(half)))).astype(np.float32)
    cf[:, 384] = inv[m % 32]
    cf[:, 385] = np.where(m % 64 < 32, -1.0, 1.0)
    cf[:, 386] = EPS
    return cbf.astype(bf), cf


def _negmask(first):
    bf = ml_dtypes.bfloat16
    kj = np.arange(128)[:, None]
    qi = np.arange(128)[None, :]
    NEG = -30000.0
    prev = np.where(kj >= qi, 0.0, NEG).astype(np.float32)
    cur = np.where(kj <= qi, 0.0, NEG).astype(np.float32)
    prev_first = np.full((128, 128), NEG, np.float32) if first else prev
    return np.concatenate([prev, cur, prev_first, cur], axis=1).astype(bf)


def _col(v):
    v = np.asarray(v, np.float32)
    return v.reshape(-1, 128).T


def make_in_maps(inputs):
    f32 = lambda a: np.ascontiguousarray(np.asarray(a, dtype=np.float32))
    x = f32(inputs["x"])
    c = f32(inputs["c"])
    pos = np.asarray(inputs["positions"]).astype(np.int32)
    cbf, cf = _consts()
    vec = np.zeros((DEPTH, 128, NV), np.float32)
    for l in range(DEPTH):
        vec[l, :, 0:8] = _col(inputs["norm_g"][l])
        vec[l, :, 8:16] = _col(inputs["conv_b"][l])
        cw = np.asarray(inputs["conv_w"][l], np.float32)
        for cc in range(8):
            for k in range(4):
                vec[l, :, 16 + cc * 4 + k] = cw[k, cc * 128:(cc + 1) * 128]
        vec[l, :, 48:56] = _col(inputs["lru_ba"][l])
        vec[l, :, 56:64] = _col(inputs["lru_bx"][l])
        vec[l, :, 64:72] = _col(inputs["lru_lambda"][l])
        vec[l, :, 72:75] = np.tile(np.asarray(inputs["qn_g"][l], np.float32).T, (2, 1))
        vec[l, :, 75:78] = np.tile(np.asarray(inputs["kn_g"][l], np.float32).T, (2, 1))
        vec[l, :, 78:82] = _col(inputs["gla_ab"][l])
        vec[l, :, 82:84] = _col(inputs["gla_on_g"][l])
        vec[l, :, 84:108] = _col(inputs["ada_b"][l])
    shared = {
        "cbf": cbf, "cf": cf, "vec": vec,
        "lru_wa": f32(inputs["lru_wa"]), "lru_wx": f32(inputs["lru_wx"]),
        "gla_a1": f32(inputs["gla_a1"]), "gla_a2": f32(inputs["gla_a2"]),
    }
    wl = [np.concatenate([f32(inputs[n][l]).reshape(-1) for n, _ in WSIZES]).reshape(NCORE, WROWS // DEPTH // NCORE, 2048)
          for l in range(DEPTH)]
    wsh = np.concatenate(wl, axis=1)
    in_maps = []
    for cidx in range(NCORE):
        b, qd = cidx // 4, cidx % 4
        sl = slice(qd * T, (qd + 1) * T)
        xT = np.ascontiguousarray(x[b, sl, :].T)
        xh = np.zeros((128, 8, 4), np.float32)
        if qd > 0:
            hx = x[b, qd * T - 3:qd * T, :]
            xh[:, :, 0:3] = hx.T.reshape(8, 128, 3).transpose(1, 0, 2)
        flags = np.zeros((128, 24), np.float32)
        flags[:, 0] = 1.0 if qd > 0 else 0.0
        if qd > 0:
            flags[:, 16 + cidx - 1] = 1.0
        for j in range(NCORE):
            flags[:, 8 + j] = 1.0 if (j // 4 == b and j < cidx) else 0.0
        m = dict(shared)
        m.update({
            "xT": xT, "xh0": xh.reshape(128, 32),
            "posb": np.ascontiguousarray(np.broadcast_to(pos[b, sl][None, :], (128, T))).astype(np.int32),
            "cT": np.ascontiguousarray(_col(c[b])),
            "flags": flags,
            "idxp": np.array([[max(cidx - 1, 0)]], np.int32),
            "negm": _negmask(qd == 0),
            "wshard": wsh[cidx],
        })
        in_maps.append(m)
    return in_maps


_NC_CACHE = {}


def kernel(**inputs):
    if "nc" not in _NC_CACHE:
        _NC_CACHE["nc"] = build_program()
    nc = _NC_CACHE["nc"]
    in_maps = make_in_maps(inputs)
    res = run_bass_kernel_spmd(nc, in_maps, core_ids=list(range(NCORE)))
    out = np.zeros((2, 4 * T, D), np.float32)
    for cidx in range(NCORE):
        b, qd = cidx // 4, cidx % 4
        out[b, qd * T:(qd + 1) * T, :] = np.asarray(res.results[cidx]["yT"]).T
    return out
```

```python
from contextlib import ExitStack
import types
import numpy as np
import ml_dtypes
import concourse.bass as bass
import concourse.mybir as mybir
from concourse.bass_utils import run_bass_kernel_spmd

F32 = mybir.dt.float32
BF16 = mybir.dt.bfloat16
I32 = mybir.dt.int32
ALU = mybir.AluOpType
AF = mybir.ActivationFunctionType

NCORE = 8
D = 1024
T = 2048
NT = 4
TT = 512
DEPTH = 2
EPS = 1e-6
NIN = 13312
O_AX, O_AG, O_BQ, O_BK, O_BV, O_BG, O_CQ, O_CK, O_CV, O_CG, O_MG = (
    0, 1024, 2048, 3584, 5120, 6656, 7168, 7680, 8192, 9216, 10240)
DILS = (1, 4, 16)
KVW = 4 * 2 * 128 * 21
FINW = 32 + 1024
WSIZES = (("w_in", D * NIN), ("ada_w", D * 3 * D), ("proj_a", D * D), ("proj_b", 512 * D), ("proj_c", D * D), ("w_o", D * D))
WOFF = {}
_o = 0
for _n, _s in WSIZES:
    WOFF[_n] = _o
    _o += _s
WPER = _o
WROWS = DEPTH * WPER // 2048
assert DEPTH * WPER % (2048 * NCORE) == 0
NV = 108
ENGS = ("sp", "act", "dve", "pool", "pe")


def kv_off(g, hp, kv):
    off = 0
    for gg in range(g):
        off += 4 * 2 * 128 * DILS[gg]
    return off + (hp * 2 + kv) * 128 * DILS[g]


def _freeze(fn):
    if getattr(fn, "__closure__", None) is None:
        return fn
    cells = []
    for c in fn.__closure__:
        try:
            cells.append(types.CellType(c.cell_contents))
        except ValueError:
            cells.append(c)
    return types.FunctionType(fn.__code__, fn.__globals__, fn.__name__, fn.__defaults__, tuple(cells))


class Buf:
    __slots__ = ("name", "h", "w", "r")

    def __init__(self, name, h):
        self.name = name
        self.h = h
        self.w = {}
        self.r = {}

    def __getitem__(self, idx):
        return self.h[idx]


class Prog:
    NDMASEM = 16

    def __init__(self, nc):
        self.nc = nc
        self.q = {e: [] for e in ENGS}
        self.cnt = {e: 0 for e in ENGS}
        self.marks = {e: set() for e in ENGS}
        self.waited = {e: {} for e in ENGS}
        self.sem = {e: nc.alloc_semaphore(name=f"c_{e}") for e in ENGS}
        self.dpool, self.dpos, self.dcnt = {}, {}, {}
        for e in ("sp", "act", "pool"):
            self.dpool[e] = [nc.alloc_semaphore(name=f"d_{e}{i}") for i in range(self.NDMASEM)]
            self.dpos[e] = 0
            self.dcnt[e] = [0] * self.NDMASEM
        self.dynoff = None
        self.idx_ap = None
        self.nsb = 0

    def sb(self, es, name, shape, dt):
        self.nsb += 1
        name = f"s{self.nsb}_{name}"
        return Buf(name, es.enter_context(self.nc.sbuf_tensor(name, list(shape), dt)))

    def ps(self, name, shape, dt=F32):
        return Buf(name, self.nc.alloc_psum_tensor(name, list(shape), dt))

    def dram(self, name, shape, dt, kind="Internal"):
        return Buf(name, self.nc.dram_tensor(name, list(shape), dt, kind=kind))

    def _wait(self, eng, tok):
        kind, key, val = tok
        if kind == "c":
            if key == "pe" and eng == "pe":
                return
            w = self.waited[eng]
            if w.get(key, 0) >= val:
                return
            w[key] = val
            self.marks[key].add(val)
            self.q[eng].append(("wc", key, val))
        else:
            w = self.waited[eng]
            k = id(key)
            if w.get(k, 0) >= val:
                return
            w[k] = val
            self.q[eng].append(("wd", key, val))

    def _deps(self, eng, reads, writes):
        for b in reads:
            for tok in b.w.values():
                self._wait(eng, tok)
        for b in writes:
            for tok in b.w.values():
                self._wait(eng, tok)
            for tok in b.r.values():
                self._wait(eng, tok)

    def _commit(self, tok, reads, writes):
        k = tok[1] if tok[0] == "c" else id(tok[1])
        for b in writes:
            b.w = {k: tok}
            b.r = {}
        for b in reads:
            if b in writes:
                continue
            old = b.r.get(k)
            if old is None or old[2] < tok[2]:
                b.r[k] = tok

    def op(self, eng, fn, reads=(), writes=()):
        self._deps(eng, reads, writes)
        self.cnt[eng] += 1
        n = self.cnt[eng]
        self.q[eng].append(("op", _freeze(fn), n))
        self._commit(("c", eng, n), reads, writes)

    def dma(self, eng, fns, reads=(), writes=()):
        if not isinstance(fns, (list, tuple)):
            fns = [fns]
        self._deps(eng, reads, writes)
        i = self.dpos[eng]
        self.dpos[eng] = (i + 1) % self.NDMASEM
        sem = self.dpool[eng][i]
        prev = self.dcnt[eng][i]
        if prev:
            self._wait(eng, ("d", sem, prev))
        val = prev + 16 * len(fns)
        self.dcnt[eng][i] = val
        for fn in fns:
            self.q[eng].append(("dma", _freeze(fn), sem))
        self._commit(("d", sem, val), reads, writes)

    def coll(self, fn, reads=(), writes=(), key="w"):
        if not hasattr(self, "csem"):
            self.csem, self.ccnt = {}, {}
        if key not in self.csem:
            self.csem[key] = self.nc.alloc_semaphore(name=f"c_coll_{key}")
            self.ccnt[key] = 0
        self._deps("pool", reads, writes)
        if self.ccnt[key]:
            self._wait("pool", ("d", self.csem[key], self.ccnt[key]))
        self.ccnt[key] += 1
        self.q["pool"].append(("coll", _freeze(fn), self.csem[key]))
        self._commit(("d", self.csem[key], self.ccnt[key]), reads, writes)

    def barrier(self, coll=True):
        toks = [("c", e, self.cnt[e]) for e in ENGS if self.cnt[e]]
        if coll:
            for key, v in getattr(self, "ccnt", {}).items():
                if v:
                    toks.append(("d", self.csem[key], v))
        for e in ("sp", "act", "pool"):
            for i, v in enumerate(self.dcnt[e]):
                if v:
                    toks.append(("d", self.dpool[e][i], v))
        for e in ENGS:
            for tok in toks:
                if tok[0] == "c" and tok[1] == e:
                    continue
                self._wait(e, tok)

    def _replay(self, eng, e):
        rank = {}
        for E in ENGS:
            rank[E] = {idx: i + 1 for i, idx in enumerate(sorted(self.marks[E]))}
        mine = self.marks[eng]
        pend = []

        def flush(ins_fn, attach=True):
            n_alone = len(pend) - 1 if (attach and pend) else len(pend)
            for sem, val in pend[:n_alone]:
                e.wait_ge(sem, val)
            ins = ins_fn()
            if attach and pend:
                ins._wait_ge(pend[-1][0], pend[-1][1])
            pend.clear()
            return ins

        for item in self.q[eng]:
            kind = item[0]
            if kind == "wc":
                pend.append((self.sem[item[1]], rank[item[1]][item[2]]))
            elif kind == "wd":
                pend.append((item[1], item[2]))
            elif kind == "op":
                ins = flush(lambda: item[1](e))
                if item[2] in mine:
                    ins.then_inc(self.sem[eng], 1)
            elif kind == "dma":
                flush(lambda: item[1](e), attach=False).then_inc(item[2], 16)
            elif kind == "coll":
                flush(lambda: item[1](e), attach=False).then_inc(item[2], 1)
        for sem, val in pend:
            e.wait_ge(sem, val)

    def emit(self):
        nc = self.nc
        with nc.Block() as block:
            @block.sync
            def _(e):
                self._replay("sp", e)

            @block.scalar
            def _(e):
                self._replay("act", e)

            @block.vector
            def _(e):
                self._replay("dve", e)

            @block.gpsimd
            def _(e):
                self._replay("pool", e)

            @block.tensor
            def _(e):
                self._replay("pe", e)


def build_program(debug=None, layers=(0, 1)):
    nc = bass.Bass("TRN2", target_bir_lowering=False)
    P = Prog(nc)
    EI = "ExternalInput"
    xT_d = P.dram("xT", [D, T], F32, EI)
    xh0_d = P.dram("xh0", [128, 32], F32, EI)
    pos_d = P.dram("posb", [128, T], I32, EI)
    cT_d = P.dram("cT", [128, 8], F32, EI)
    flags_d = P.dram("flags", [128, 24], F32, EI)
    idx_d = P.dram("idxp", [1, 1], I32, EI)
    negm_d = P.dram("negm", [128, 512], BF16, EI)
    cbf_d = P.dram("cbf", [128, 640 + 2 * T], BF16, EI)
    cf_d = P.dram("cf", [128, 3 * 128 + 4], F32, EI)
    vec_d = P.dram("vec", [DEPTH, 128, NV], F32, EI)
    lru_wa_d = P.dram("lru_wa", [DEPTH, 16, 64, 64], F32, EI)
    lru_wx_d = P.dram("lru_wx", [DEPTH, 16, 64, 64], F32, EI)
    a1_d = P.dram("gla_a1", [DEPTH, D, 16], F32, EI)
    a2_d = P.dram("gla_a2", [DEPTH, 16, 512], F32, EI)
    wshard_d = P.dram("wshard", [WROWS // NCORE, 2048], F32, EI)
    LR = WROWS // DEPTH
    wfull_l = [P.dram(f"wfull{l}", [LR, 2048], BF16) for l in range(DEPTH)]
    wbounce_l = [P.dram(f"wbounce{l}", [LR // NCORE, 2048], BF16) for l in range(DEPTH)]

    class WView:
        def __init__(self, name, rows, cols):
            self.name, self.rows, self.cols = name, rows, cols
            self.h = self

        def ap(self):
            return self

        def __getitem__(self, idx):
            l, rs, cs = idx
            off = WOFF[self.name]
            flat = wfull_l[l].h.ap().rearrange("a b -> (a b)")[off:off + self.rows * self.cols]
            return flat.rearrange("(r c) -> r c", c=self.cols)[rs, cs]

    w_in_d = WView("w_in", D, NIN)
    ada_w_d = WView("ada_w", D, 3 * D)
    proj_a_d = WView("proj_a", D, D)
    proj_b_d = WView("proj_b", 512, D)
    proj_c_d = WView("proj_c", D, D)
    w_o_d = WView("w_o", D, D)
    yT_d = P.dram("yT", [D, T], F32, "ExternalOutput")
    P.idx_ap = idx_d[0:1, 0:1]
    dbg_kind = "Internal"
    x1_d = P.dram("x1", [D, T], F32, "ExternalOutput" if debug else "Internal")
    ya_d = P.dram("ya_s", [D, T], BF16, dbg_kind)
    ag_d = P.dram("ag_s", [D, T], BF16, dbg_kind)
    q_d = P.dram("q_s", [12, 128, T], BF16, dbg_kind)
    k_d = P.dram("k_s", [12, 128, T], BF16, dbg_kind)
    v_d = P.dram("v_s", [12, 128, T], BF16, dbg_kind)
    qg_d = P.dram("qg_s", [512, T], BF16, dbg_kind)
    ol_d = P.dram("ol_s", [D, T], F32, dbg_kind)
    tk_d = [P.dram(f"tk{i}_s", [D, T], F32, dbg_kind) for i in range(3)]
    kvsend = P.dram("kvsend", [128, KVW], BF16)
    kvrecv = P.dram("kvrecv", [NCORE * 128, KVW], BF16)
    kvprev = P.dram("kvprev", [128, KVW], BF16)
    finsend = P.dram("finsend", [128, FINW], F32)
    finrecv = P.dram("finrecv", [NCORE * 128, FINW], F32)
    xhsend = P.dram("xhsend", [128, 32], F32)
    xhrecv = P.dram("xhrecv", [NCORE * 128, 32], F32)
    dbg = {}
    if debug:
        dbg["ybT"] = P.dram("dbg_ybT", [512, T], BF16, "ExternalOutput")
        dbg["yaT"] = P.dram("dbg_yaT", [D, T], BF16, "ExternalOutput")
        dbg["ycT"] = P.dram("dbg_ycT", [D, T], BF16, "ExternalOutput")
        dbg["hin"] = P.dram("dbg_hin", [128, 8], F32, "ExternalOutput")
        dbg["finr"] = P.dram("dbg_finr", [128, 256], F32, "ExternalOutput")

    PSB = [P.ps(f"psb{i}", [128, 512], F32) for i in range(7)]
    PSH = P.ps("psh", [128, 1024], BF16)
    bank_i = [0]

    def bank():
        b = PSB[bank_i[0] % 7]
        bank_i[0] += 1
        return b

    def MM(out, lhsT, rhs, start, stop):
        return lambda e: e.matmul(out, lhsT, rhs, start=start, stop=stop)

    def ACT(out, in_, func, bias=None, scale=None):
        kw = {}
        if bias is not None:
            kw["bias"] = bias
        if scale is not None:
            kw["scale"] = scale
        return lambda e: e.activation(out, in_, func, **kw)

    def TT_(out, a, b, op):
        return lambda e: e.tensor_tensor(out, a, b, op)

    def TS(out, a, s1, s2, op0, op1=None):
        if op1 is None:
            return lambda e: e.tensor_scalar(out, a, s1, None, op0)
        return lambda e: e.tensor_scalar(out, a, s1, s2, op0, op1)

    def STT(out, a, s, b, op0, op1):
        return lambda e: e.scalar_tensor_tensor(out, a, s, b, op0, op1)

    def CP(out, in_):
        return lambda e: e.tensor_copy(out, in_)

    def DMA(out, in_):
        return lambda e: e.dma_start(out=out, in_=in_)

    class Ring:
        def __init__(self, es, name, shape, dt, n):
            self.b = [P.sb(es, f"{name}{i}", shape, dt) for i in range(n)]
            self.i = 0

        def next(self):
            b = self.b[self.i % len(self.b)]
            self.i += 1
            return b

    with ExitStack() as glob:
        cbf = P.sb(glob, "cbf", [128, 640 + 2 * T], BF16)
        cf = P.sb(glob, "cf", [128, 388], F32)
        flags = P.sb(glob, "flags", [128, 24], F32)
        negm = P.sb(glob, "negm", [128, 512], BF16)
        vec = P.sb(glob, "vec", [128, DEPTH, NV], F32)
        hT = P.sb(glob, "hT", [128, 8, T], BF16)
        hTh = P.sb(glob, "hTh", [128, 8, 4], BF16)
        der = P.sb(glob, "der", [128, DEPTH, 64], F32)
        P.dma("sp", DMA(cbf[:], cbf_d[:]), [cbf_d], [cbf])
        P.dma("sp", DMA(cf[:], cf_d[:]), [cf_d], [cf])
        P.dma("sp", DMA(flags[:], flags_d[:]), [flags_d], [flags])
        P.dma("sp", DMA(negm[:], negm_d[:]), [negm_d], [negm])
        P.dma("sp", DMA(vec[:], vec_d.h.ap().rearrange("l p n -> p l n")), [vec_d], [vec])
        WR = LR // NCORE
        NPP = WR * 2048 // 128
        PIECE = NPP // 4
        with ExitStack() as es:
            wcf = [P.sb(es, f"wcf{i}", [128, PIECE], F32) for i in range(2)]
            wcb = [P.sb(es, f"wcb{i}", [128, PIECE], BF16) for i in range(2)]
            ci_ = 0
            for l in range(DEPTH):
                src = wshard_d.h.ap()[l * WR:(l + 1) * WR, :].rearrange("a b -> (a b)").rearrange("(p n) -> p n", p=128)
                dst = wbounce_l[l].h.ap().rearrange("a b -> (a b)").rearrange("(p n) -> p n", p=128)
                for pc in range(4):
                    f_, b_ = wcf[ci_ % 2], wcb[ci_ % 2]
                    P.dma("sp", DMA(f_[:], src[:, pc * PIECE:(pc + 1) * PIECE]), [wshard_d], [f_])
                    if ci_ % 2 == 0:
                        P.op("dve", CP(b_[:], f_[:]), [f_], [b_])
                    else:
                        P.op("act", lambda e, b_=b_, f_=f_: e.copy(b_[:], f_[:]), [f_], [b_])
                    P.dma("sp", DMA(dst[:, pc * PIECE:(pc + 1) * PIECE], b_[:]), [b_], [wbounce_l[l]])
                    ci_ += 1
                P.coll(lambda e, l=l: e.collective_compute("AllGather", ALU.bypass, replica_groups=[list(range(NCORE))],
                                                           ins=[wbounce_l[l].h.ap().opt()], outs=[wfull_l[l].h.ap().opt()]), [wbounce_l[l]], [wfull_l[l]], key=f"w{l}")
            P.barrier(coll=False)
        ident = cbf[:, 0:128]
        rperm = cbf[:, 128:256]
        maskpair = cbf[:, 256:384]
        onesb = cbf[:, 384:512]
        resetm = cbf[:, 640:640 + T]
        onesT = cbf[:, 640 + T:640 + 2 * T]
        mean1024 = cf[:, 0:128]
        mean64 = cf[:, 128:256]
        mean256 = cf[:, 256:384]

        def V(l, a, b):
            return vec[:, l, a:b]

        with ExitStack() as es:
            cact = P.sb(es, "cact", [128, 8], F32)
            mod = P.sb(es, "mod", [128, 24], F32)
            tmpc = P.sb(es, "tmpc", [128, 8], F32)
            P.dma("sp", DMA(cact[:], cT_d[:]), [cT_d], [cact])
            P.op("act", ACT(cact[:], cact[:], AF.Silu), [cact], [cact])
            aw = P.sb(es, "aw", [128, 8, 1024], BF16)
            cactb = P.sb(es, "cactb", [128, 8], BF16)
            P.op("dve", CP(cactb[:], cact[:]), [cact], [cactb])
            for l in range(DEPTH):
                for j3 in range(3):
                    P.dma("sp", [DMA(aw[:, 0:4, :], ada_w_d.h.ap()[l, 0:512, j3 * 1024:(j3 + 1) * 1024].rearrange("(k p) n -> p k n", p=128)),
                                 DMA(aw[:, 4:8, :], ada_w_d.h.ap()[l, 512:1024, j3 * 1024:(j3 + 1) * 1024].rearrange("(k p) n -> p k n", p=128))],
                          [wfull_l[l]], [aw])
                    pb = bank()
                    for j in range(8):
                        for k in range(8):
                            P.op("pe", MM(pb[:, j:j + 1], aw[:, k, j * 128:(j + 1) * 128], cactb[:, k:k + 1], k == 0, k == 7),
                                 [aw, cactb], [pb])
                    P.op("dve", TT_(mod[:, j3 * 8:(j3 + 1) * 8], pb[:, 0:8], V(l, 84 + j3 * 8, 92 + j3 * 8), ALU.add), [pb, vec], [mod])
                P.op("dve", TS(tmpc[:], mod[:, 8:16], 1.0, None, ALU.add), [mod], [tmpc])
                P.op("dve", TT_(der[:, l, 0:8], tmpc[:], V(l, 0, 8), ALU.mult), [tmpc, vec], [der])
                P.op("dve", CP(der[:, l, 8:16], mod[:, 0:8]), [mod], [der])
                P.op("dve", CP(der[:, l, 16:24], mod[:, 16:24]), [mod], [der])
                P.op("act", ACT(tmpc[:], V(l, 64, 72), AF.Exp, scale=-1.0), [vec], [tmpc])
                P.op("act", ACT(tmpc[:], tmpc[:], AF.Ln, bias=1.0), [tmpc], [tmpc])
                P.op("dve", TS(der[:, l, 24:32], tmpc[:], -8.0, None, ALU.mult), [tmpc], [der])
                P.op("dve", TS(der[:, l, 32:40], tmpc[:], -16.0, None, ALU.mult), [tmpc], [der])
                P.op("dve", TS(der[:, l, 40:43], V(l, 72, 75), 0.125, None, ALU.mult), [vec], [der])
                P.op("dve", TS(der[:, l, 44:48], V(l, 78, 82), -1.0, None, ALU.mult), [vec], [der])
            P.barrier(coll=False)

        WREADS = []

        class Slabs:
            def __init__(self, es, nst=1, nbf=2, tag="", st=None):
                self.st = []
                self.bf = [P.sb(es, f"wbf{tag}{i}", [128, 8, 512], BF16) for i in range(nbf)]
                self.bi = 0

            def load(self, src_ap, nk=8, width=512, dma_eng="sp"):
                bfb = self.bf[self.bi % len(self.bf)]
                self.bi += 1
                v = src_ap.rearrange("(k p) n -> p k n", p=128)
                h = max(nk // 2, 1)
                fns = [DMA(bfb[:, 0:h, 0:width], v[:, 0:h, :])]
                if nk > h:
                    fns.append(DMA(bfb[:, h:nk, 0:width], v[:, h:nk, :]))
                P.dma(dma_eng, fns, WREADS, [bfb])
                return bfb

        def proj(pb_ap, wb, c0, ncols, tt, buf_pb, n=TT, rhs_fn=None):
            for k in range(8):
                rhs = hT[:, k, tt * TT:tt * TT + n] if rhs_fn is None else rhs_fn(k)
                P.op("pe", MM(pb_ap, wb[:, k, c0:c0 + ncols], rhs, k == 0, k == 7), [wb, hT, hTh], [buf_pb])

        for l in layers:
            xsrc = xT_d if l == 0 else x1_d
            WREADS[:] = [wfull_l[l]]
            xdst = x1_d if l == 0 else yT_d
            Acol = lambda k: der[:, l, k:k + 1]
            Bcol = lambda k: der[:, l, 8 + k:9 + k]

            with ExitStack() as es:
                xts = [P.sb(es, f"xt{i}", [128, 8, TT], F32) for i in range(2)]
                sq = P.sb(es, "sq", [128, 8, TT], F32)
                rstd = P.sb(es, "rstd", [128, TT], F32)
                xh = P.sb(es, "xh", [128, 8, 4], F32)
                xhf = P.sb(es, "xhf", [128, 32], F32)
                xhr = P.sb(es, "xhr", [128, 8, 32], F32)
                for tt in range(NT + 1):
                    halo = tt == NT
                    n = 4 if halo else TT
                    if halo:
                        xt = xh
                        if l == 0:
                            P.dma("sp", DMA(xhf[:], xh0_d[:, :]), [xh0_d], [xh, xhf])
                        else:
                            for j in range(NCORE):
                                P.dma("sp", DMA(xhr[:, j, :], xhrecv[j * 128:(j + 1) * 128, :]), [xhrecv], [xhr])
                            P.op("dve", TS(xhf[:], xhr[:, 0, :], flags[:, 16:17], None, ALU.mult), [xhr, flags], [xhf])
                            for j in range(1, NCORE):
                                P.op("dve", STT(xhf[:], xhr[:, j, :], flags[:, 16 + j:17 + j], xhf[:], ALU.mult, ALU.add), [xhr, flags, xhf], [xhf])
                        P.op("dve", CP(xh[:], xhf[:].rearrange("p (k n) -> p k n", n=4)), [xhf], [xh])
                    else:
                        xt = xts[tt % 2]
                        v = xsrc.h.ap()[:, tt * TT:(tt + 1) * TT].rearrange("(k p) n -> p k n", p=128)
                        P.dma("sp", [DMA(xt[:, 0:4, :], v[:, 0:4, :]), DMA(xt[:, 4:8, :], v[:, 4:8, :])], [xsrc], [xt])
                    P.op("act", ACT(sq[:, :, 0:n], xt[:, :, 0:n], AF.Square), [xt], [sq])
                    pb = bank()
                    for k in range(8):
                        P.op("pe", MM(pb[:, 0:n], mean1024, sq[:, k, 0:n], k == 0, k == 7), [cf, sq], [pb])
                    P.op("act", ACT(rstd[:, 0:n], pb[:, 0:n], AF.Ln, bias=cf[:, 386:387]), [pb, cf], [rstd])
                    P.op("act", ACT(rstd[:, 0:n], rstd[:, 0:n], AF.Exp, scale=-0.5), [rstd], [rstd])
                    P.op("dve", TT_(xt[:, :, 0:n], xt[:, :, 0:n], rstd[:, 0:n].unsqueeze(1).to_broadcast([128, 8, n]), ALU.mult), [xt, rstd], [xt])
                    for k in range(8):
                        dst = hTh[:, k, :] if halo else hT[:, k, tt * TT:(tt + 1) * TT]
                        P.op("act", ACT(dst, xt[:, k, 0:n], AF.Identity, bias=Bcol(k), scale=Acol(k)), [xt, der], [hTh if halo else hT])
                P.barrier(coll=False)
            if debug == "B":
                break

            with ExitStack() as es:
                S = Slabs(es, nst=1, nbf=3)
                cosT = P.sb(es, "cosT", [128, T], F32)
                sinS = P.sb(es, "sinS", [128, T], F32)
                with ExitStack() as es2:
                    pi_ = P.sb(es2, "pos_i", [128, T], I32)
                    pf = P.sb(es2, "pos_f", [128, T], F32)
                    ang = P.sb(es2, "ang", [128, T], F32)
                    kf = P.sb(es2, "kf", [128, T], F32)
                    ki = P.sb(es2, "ki", [128, T], I32)
                    P.dma("sp", DMA(pi_[:], pos_d[:]), [pos_d], [pi_])
                    P.op("dve", CP(pf[:], pi_[:]), [pi_], [pf])
                    C1 = 6.28125
                    C2 = float(2 * np.pi - 6.28125)
                    for which, shift, dst in (("sin", 0.0, sinS), ("cos", float(np.pi / 2), cosT)):
                        P.op("dve", TS(ang[:], pf[:], cf[:, 384:385], shift, ALU.mult, ALU.add), [pf, cf], [ang])
                        P.op("dve", TS(kf[:], ang[:], float(1 / (2 * np.pi)), 0.5, ALU.mult, ALU.add), [ang], [kf])
                        P.op("dve", CP(ki[:], kf[:]), [kf], [ki])
                        P.op("dve", CP(kf[:], ki[:]), [ki], [kf])
                        P.op("dve", STT(ang[:], kf[:], -C1, ang[:], ALU.mult, ALU.add), [kf, ang], [ang])
                        P.op("dve", STT(ang[:], kf[:], -C2, ang[:], ALU.mult, ALU.add), [kf, ang], [ang])
                        P.op("dve", lambda e: e.tensor_single_scalar(kf[:], ang[:], float(-np.pi), ALU.is_lt), [ang], [kf])
                        P.op("dve", STT(ang[:], kf[:], float(2 * np.pi), ang[:], ALU.mult, ALU.add), [kf, ang], [ang])
                        P.op("dve", lambda e: e.tensor_single_scalar(kf[:], ang[:], float(np.pi), ALU.is_gt), [ang], [kf])
                        P.op("dve", STT(ang[:], kf[:], float(-2 * np.pi), ang[:], ALU.mult, ALU.add), [kf, ang], [ang])
                        P.op("act", ACT(dst[:], ang[:], AF.Sin), [ang], [dst])
                    P.op("dve", TS(sinS[:], sinS[:], cf[:, 385:386], None, ALU.mult), [sinS, cf], [sinS])

                    P.barrier(coll=False)
                sqt_r = Ring(es, "sqt", [128, TT], F32, 3)
                rs_r = Ring(es, "rs", [128, TT], F32, 3)
                xn_r = Ring(es, "xn", [128, TT], F32, 3)
                xnb_r = Ring(es, "xnb", [128, TT], BF16, 3)
                t1_r = Ring(es, "t1", [128, TT], F32, 3)
                t2_r = Ring(es, "t2", [128, TT], F32, 3)
                qko = [P.sb(es, f"qko{i}", [128, T], BF16) for i in range(3)]
                vst = P.sb(es, "vst", [128, 4, 16, 128], BF16)
                for g in range(3):
                    d = DILS[g]
                    for which in range(2):
                        wsl = S.load(w_in_d.h.ap()[l, :, (O_BQ if which == 0 else O_BK) + g * 512:(O_BQ if which == 0 else O_BK) + (g + 1) * 512])
                        gcol = der[:, l, 40 + g:41 + g] if which == 0 else vec[:, l, 75 + g:76 + g]
                        for hp in range(4):
                            out = qko[(which * 4 + hp) % 3]
                            for tt in range(NT):
                                sl = slice(tt * TT, (tt + 1) * TT)
                                pb = bank()
                                proj(pb[:, :], wsl, hp * 128, 128, tt, pb)
                                sqt, rs, xn, xnb, t1, t2 = sqt_r.next(), rs_r.next(), xn_r.next(), xnb_r.next(), t1_r.next(), t2_r.next()
                                P.op("act", ACT(sqt[:], pb[:, :], AF.Square), [pb], [sqt])
                                pn = bank()
                                P.op("pe", MM(pn[:, :], mean64, sqt[:], True, True), [cf, sqt], [pn])
                                P.op("act", ACT(rs[:], pn[:, :], AF.Ln, bias=cf[:, 386:387]), [pn, cf], [rs])
                                P.op("act", ACT(rs[:], rs[:], AF.Exp, scale=-0.5), [rs], [rs])
                                P.op("dve", STT(xn[:], pb[:, :], gcol, rs[:], ALU.mult, ALU.mult), [pb, der, vec, rs], [xn])
                                P.op("act", lambda e, xnb=xnb, xn=xn: e.copy(xnb[:], xn[:]), [xn], [xnb])
                                pr = bank()
                                P.op("pe", MM(pr[:, :], rperm, xnb[:], True, True), [cbf, xnb], [pr])
                                P.op("pool", TT_(t1[:], xn[:], cosT[:, sl], ALU.mult), [xn, cosT], [t1])
                                P.op("dve", TT_(t2[:], pr[:, :], sinS[:, sl], ALU.mult), [pr, sinS], [t2])
                                P.op("pool", TT_(out[:, sl], t1[:], t2[:], ALU.add), [t1, t2], [out])
                            dst = q_d if which == 0 else k_d
                            P.dma("sp", DMA(dst[g * 4 + hp], out[:]), [out], [dst])
                            if which == 1:
                                o = kv_off(g, hp, 0)
                                P.dma("sp", DMA(kvsend[:, o:o + 128 * d], out[:, T - 128 * d:T]), [out], [kvsend])
                    wvs = S.load(w_in_d.h.ap()[l, :, O_BV + g * 512:O_BV + (g + 1) * 512])
                    for blk in range(16):
                        n_, r_ = blk // d, blk % d
                        t0 = n_ * 128 * d + r_
                        pb = bank()
                        for k in range(8):
                            P.op("pe", MM(pb[:, :], hT[:, k, t0:t0 + 127 * d + 1:d], wvs[:, k, :], k == 0, k == 7), [hT, wvs], [pb])
                        P.op("act", lambda e, pb=pb, blk=blk: e.copy(vst[:, :, blk, :], pb[:, :].rearrange("p (a b) -> p a b", b=128)), [pb], [vst])
                    for hp in range(4):
                        P.dma("sp", DMA(v_d[g * 4 + hp].rearrange("p (a b) -> p a b", b=128), vst[:, hp, :, :]), [vst], [v_d])
                        o = kv_off(g, hp, 1)
                        P.dma("sp", DMA(kvsend[:, o:o + 128 * d].rearrange("p (a b) -> p a b", b=128), vst[:, hp, 16 - d:16, :]), [vst], [kvsend])
                P.barrier(coll=False)

            P.coll(lambda e: e.collective_compute("AllGather", ALU.bypass, replica_groups=[list(range(NCORE))],
                                                         ins=[kvsend.h.ap().opt()], outs=[kvrecv.h.ap().opt()]), [kvsend], [kvrecv], key="kv")

            with ExitStack() as es:
                S = Slabs(es, nst=1, nbf=2)
                wst = P.sb(es, "gw_st", [128, 2, 8, 128], F32)
                wbd = P.sb(es, "gw_bd", [128, 2, 8, 128], BF16)
                xa_r = Ring(es, "xa", [128, 3 + T], F32, 2)
                xc_r = Ring(es, "xc", [128, T], F32, 2)
                xcb_r = Ring(es, "xcb", [128, T], BF16, 2)
                rt_r = Ring(es, "rt", [128, TT], F32, 3)
                it_r = Ring(es, "it", [128, TT], F32, 3)
                a2t_r = Ring(es, "a2t", [128, TT], F32, 3)
                af = P.sb(es, "af", [128, T], F32)
                uf = P.sb(es, "uf", [128, T], F32)
                hl = P.sb(es, "hl", [128, T], F32)
                At = P.sb(es, "At", [128, T], F32)
                sg = P.sb(es, "sg", [128, T], F32)
                yo = P.sb(es, "yo", [128, T], BF16)
                ao = P.sb(es, "ao", [128, T], BF16)
                fin = P.sb(es, "fin", [128, 20], F32)
                P.op("pool", lambda e: e.memset(wst[:], 0.0), [], [wst])
                for wi, wd in enumerate((lru_wa_d, lru_wx_d)):
                    fns = []
                    for half in range(2):
                        src = wd.h.ap()[l].rearrange("(c t) j k -> t j c k", t=2)[half]
                        fns.append(DMA(wst[half * 64:(half + 1) * 64, wi, :, half * 64:(half + 1) * 64], src))
                    P.dma("sp", fns, [wd], [wst])
                P.op("pool", CP(wbd[:], wst[:]), [wst], [wbd])
                for s4 in range(2):
                    wax = S.load(w_in_d.h.ap()[l, :, O_AX + s4 * 512:O_AX + (s4 + 1) * 512])
                    wag = S.load(w_in_d.h.ap()[l, :, O_AG + s4 * 512:O_AG + (s4 + 1) * 512])
                    for c4 in range(4):
                        cc = s4 * 4 + c4
                        xa, xc, xcb = xa_r.next(), xc_r.next(), xcb_r.next()
                        col = lambda base: vec[:, l, base + cc:base + cc + 1]
                        dcol = lambda base: der[:, l, base + cc:base + cc + 1]
                        for tt in range(NT):
                            pb = bank()
                            proj(pb[:, :], wax, c4 * 128, 128, tt, pb)
                            P.op("act", lambda e, pb=pb, tt=tt, xa=xa: e.copy(xa[:, 3 + tt * TT:3 + (tt + 1) * TT], pb[:, :]), [pb], [xa])
                        pb = bank()
                        proj(pb[:, 0:4], wax, c4 * 128, 128, 0, pb, n=4, rhs_fn=lambda k: hTh[:, k, :])
                        P.op("dve", TS(xa[:, 0:3], pb[:, 0:3], flags[:, 0:1], None, ALU.mult), [pb, flags], [xa])
                        cw = lambda k: vec[:, l, 16 + cc * 4 + k:17 + cc * 4 + k]
                        P.op("dve", TS(xc[:], xa[:, 0:T], cw(0), col(8), ALU.mult, ALU.add), [xa, vec], [xc])
                        for k in range(1, 4):
                            P.op("dve", STT(xc[:], xa[:, k:k + T], cw(k), xc[:], ALU.mult, ALU.add), [xa, vec, xc], [xc])
                        P.op("act", lambda e, xcb=xcb, xc=xc: e.copy(xcb[:], xc[:]), [xc], [xcb])
                        for tt in range(NT):
                            sl = slice(tt * TT, (tt + 1) * TT)
                            rt, it, a2t = rt_r.next(), it_r.next(), a2t_r.next()
                            pr = bank()
                            P.op("pe", MM(pr[:, :], wbd[:, 0, cc, :], xcb[:, sl], True, True), [wbd, xcb], [pr])
                            pi2 = bank()
                            P.op("pe", MM(pi2[:, :], wbd[:, 1, cc, :], xcb[:, sl], True, True), [wbd, xcb], [pi2])
                            P.op("act", ACT(rt[:], pr[:, :], AF.Sigmoid, bias=col(48)), [pr, vec], [rt])
                            P.op("act", ACT(it[:], pi2[:, :], AF.Sigmoid, bias=col(56)), [pi2, vec], [it])
                            P.op("act", ACT(af[:, sl], rt[:], AF.Exp, scale=dcol(24)), [rt, der], [af])
                            P.op("dve", TS(sg[:, sl], rt[:], dcol(24), None, ALU.mult), [rt, der], [sg])
                            P.op("act", ACT(a2t[:], rt[:], AF.Exp, scale=dcol(32)), [rt, der], [a2t])
                            P.op("act", ACT(a2t[:], a2t[:], AF.Sqrt, bias=1.0, scale=-1.0), [a2t], [a2t])
                            P.op("dve", TT_(it[:], it[:], xc[:, sl], ALU.mult), [it, xc], [it])
                            P.op("dve", TT_(uf[:, sl], it[:], a2t[:], ALU.mult), [it, a2t], [uf])
                        P.op("dve", lambda e: e.tensor_tensor_scan(hl[:], af[:], uf[:], 0.0, ALU.mult, ALU.add), [af, uf], [hl])
                        P.op("dve", lambda e: e.tensor_tensor_scan(At[:], onesT, sg[:], 0.0, ALU.mult, ALU.add), [cbf, sg], [At])
                        P.op("act", ACT(At[:], At[:], AF.Exp), [At], [At])
                        P.op("dve", CP(fin[:, cc:cc + 1], hl[:, T - 1:T]), [hl], [fin])
                        P.op("dve", CP(fin[:, 8 + cc:9 + cc], At[:, T - 1:T]), [At], [fin])
                        for tt in range(NT):
                            pb = bank()
                            proj(pb[:, :], wag, c4 * 128, 128, tt, pb)
                            P.op("act", ACT(sg[:, tt * TT:(tt + 1) * TT], pb[:, :], AF.Silu), [pb], [sg])
                        P.op("dve", TT_(yo[:], hl[:], sg[:], ALU.mult), [hl, sg], [yo])
                        P.op("pool", TT_(ao[:], At[:], sg[:], ALU.mult), [At, sg], [ao])
                        P.dma("sp", DMA(ya_d[cc * 128:(cc + 1) * 128, :], yo[:]), [yo], [ya_d])
                        P.dma("sp", DMA(ag_d[cc * 128:(cc + 1) * 128, :], ao[:]), [ao], [ag_d])
                P.dma("sp", DMA(finsend[:, 0:16], fin[:, 0:16]), [fin], [finsend])
                P.barrier(coll=False)

            with ExitStack() as es:
                S = Slabs(es, nst=1, nbf=2, tag="qk")
                Sv = Slabs(es, nbf=1, tag="v", st=S.st)
                a1s = P.sb(es, "a1s", [128, 8, 16], F32)
                a1b = P.sb(es, "a1b", [128, 8, 16], BF16)
                a2s = P.sb(es, "a2s", [16, 512], F32)
                a2b = P.sb(es, "a2b", [16, 512], BF16)
                la1b = P.sb(es, "la1b", [16, T], BF16)
                vtok = P.sb(es, "vtok", [128, 16, 256], BF16)
                la = P.sb(es, "la", [128, T], F32)
                eb = P.sb(es, "eb", [128, T], F32)
                eB = P.sb(es, "eB", [128, T], F32)
                kinvf = P.sb(es, "kinvf", [128, T], F32)
                decay = P.sb(es, "decay", [128, 32], F32)
                qdec = P.sb(es, "qdec", [128, T], BF16)
                qglob = P.sb(es, "qglob", [128, T], BF16)
                kinv = P.sb(es, "kinv", [128, T], BF16)
                kend = P.sb(es, "kend", [128, T], BF16)
                kendT = P.sb(es, "kendT", [128, 16, 128], BF16)
                Sst = P.sb(es, "Sst", [128, 256], F32)
                Sbs = [P.sb(es, f"Sb{i}", [128, 256], BF16) for i in range(4)]
                att = [P.sb(es, f"att{i}", [128, 128], BF16) for i in range(2)]
                ost = [P.sb(es, f"ost{i}", [128, 2, 128], F32) for i in range(3)]
                dt4 = P.sb(es, "dt4", [128, 16], F32)
                P.op("pool", lambda e: e.memset(dt4[:], 0.0), [], [dt4])
                P.dma("sp", DMA(a1s[:], a1_d.h.ap()[l].rearrange("(k p) n -> p k n", p=128)), [a1_d], [a1s])
                P.dma("sp", DMA(a2s[:], a2_d.h.ap()[l]), [a2_d], [a2s])
                P.op("pool", CP(a1b[:], a1s[:]), [a1s], [a1b])
                P.op("pool", CP(a2b[:], a2s[:]), [a2s], [a2b])
                for tt in range(NT):
                    pb = bank()
                    for k in range(8):
                        P.op("pe", MM(pb[0:16, :], a1b[:, k, :], hT[:, k, tt * TT:(tt + 1) * TT], k == 0, k == 7), [a1b, hT], [pb])
                    P.op("act", lambda e, pb=pb, tt=tt: e.copy(la1b[:, tt * TT:(tt + 1) * TT], pb[0:16, :]), [pb], [la1b])
                wq = S.load(w_in_d.h.ap()[l, :, O_CQ:O_CQ + 512])
                wk = S.load(w_in_d.h.ap()[l, :, O_CK:O_CK + 512])
                osi = 0
                for hh in range(4):
                    wvs = Sv.load(w_in_d.h.ap()[l, :, O_CV + hh * 256:O_CV + (hh + 1) * 256], width=256)
                    for p in range(16):
                        pb = bank()
                        for k in range(8):
                            P.op("pe", MM(pb[:, 0:256], hT[:, k, p * 128:(p + 1) * 128], wvs[:, k, 0:256], k == 0, k == 7), [hT, wvs], [pb])
                        if p % 2 == 0:
                            P.op("act", lambda e, pb=pb, p=p: e.copy(vtok[:, p, :], pb[:, 0:256]), [pb], [vtok])
                        else:
                            P.op("dve", CP(vtok[:, p, :], pb[:, 0:256]), [pb], [vtok])
                    for tt in range(NT):
                        sl = slice(tt * TT, (tt + 1) * TT)
                        pb = bank()
                        P.op("pe", MM(pb[:, :], a2b[:, hh * 128:(hh + 1) * 128], la1b[:, sl], True, True), [a2b, la1b], [pb])
                        P.op("act", ACT(la[:, sl], pb[:, :], AF.Exp, bias=der[:, l, 44 + hh:45 + hh], scale=-1.0), [pb, der], [la])
                    P.op("act", ACT(la[:], la[:], AF.Ln, bias=1.0), [la], [la])
                    P.op("dve", TS(la[:], la[:], -1.0 / 16.0, None, ALU.mult), [la], [la])
                    P.op("dve", lambda e: e.tensor_tensor_scan(eb[:], resetm, la[:], 0.0, ALU.mult, ALU.add), [cbf, la], [eb])
                    P.op("dve", lambda e: e.tensor_tensor_scan(eB[:], onesT, la[:], 0.0, ALU.mult, ALU.add), [cbf, la], [eB])
                    P.op("act", ACT(kinvf[:], eb[:], AF.Exp, scale=-1.0), [eb], [kinvf])
                    P.op("act", ACT(eb[:], eb[:], AF.Exp), [eb], [eb])
                    P.op("act", ACT(eB[:], eB[:], AF.Exp), [eB], [eB])
                    P.op("pool", CP(decay[:], eb[:].rearrange("p (c j) -> p c j", j=64)[:, :, 63]), [eb], [decay])
                    P.op("dve", CP(dt4[:, hh:hh + 1], eB[:, T - 1:T]), [eB], [dt4])
                    for tt in range(NT):
                        sl = slice(tt * TT, (tt + 1) * TT)
                        pb = bank()
                        proj(pb[:, :], wq, hh * 128, 128, tt, pb)
                        P.op("dve", STT(qdec[:, sl], pb[:, :], float(128 ** -0.5), eb[:, sl], ALU.mult, ALU.mult), [pb, eb], [qdec])
                        P.op("dve", STT(qglob[:, sl], pb[:, :], float(128 ** -0.5), eB[:, sl], ALU.mult, ALU.mult), [pb, eB], [qglob])
                        pb = bank()
                        proj(pb[:, :], wk, hh * 128, 128, tt, pb)
                        P.op("dve", TT_(kinvf[:, sl], pb[:, :], kinvf[:, sl], ALU.mult), [pb, kinvf], [kinvf])
                    P.dma("sp", DMA(qg_d[hh * 128:(hh + 1) * 128, :], qglob[:]), [qglob], [qg_d])
                    P.op("act", lambda e: e.copy(kinv[:], kinvf[:]), [kinvf], [kinv])
                    P.op("pool", TT_(kend[:].rearrange("p (c j) -> p c j", j=64), kinvf[:].rearrange("p (c j) -> p c j", j=64),
                                     decay[:].unsqueeze(2).to_broadcast([128, 32, 64]), ALU.mult), [kinvf, decay], [kend])
                    for p4 in range(4):
                        for pp in range(4):
                            p = p4 * 4 + pp
                            P.op("pe", lambda e, p=p, pp=pp: e.transpose(PSH[:, pp * 128:(pp + 1) * 128], kend[:, p * 128:(p + 1) * 128], ident), [kend, cbf], [PSH])
                        P.op("act", lambda e, p4=p4: e.copy(kendT[:, p4 * 4:(p4 + 1) * 4, :], PSH[:, 0:512].rearrange("p (a b) -> p a b", b=128)), [PSH], [kendT])
                    P.op("pool", lambda e: e.memset(Sst[:], 0.0), [], [Sst])
                    P.op("pool", lambda e: e.memset(Sbs[3][:], 0.0), [], [Sbs[3]])
                    sbi = 3
                    for p in range(16):
                        ts_ = slice(p * 128, (p + 1) * 128)
                        pa = bank()
                        P.op("pe", MM(pa[:, 0:128], kinv[:, ts_], qdec[:, ts_], True, True), [kinv, qdec], [pa])
                        at = att[p % 2]
                        P.op("dve", TT_(at[:], pa[:, 0:128], maskpair, ALU.mult), [pa, cbf], [at])
                        pos_ = [bank(), bank()]
                        sb_prev = Sbs[sbi % 4]
                        for half in range(2):
                            po = pos_[half]
                            P.op("pe", MM(po[:, 0:128], vtok[:, p, half * 128:(half + 1) * 128], at[:], True, False), [vtok, at], [po])
                            P.op("pe", MM(po[:, 0:64], sb_prev[:, half * 128:(half + 1) * 128], qdec[:, p * 128:p * 128 + 64], False, False), [sb_prev, qdec], [po])
                        for c2 in range(2):
                            pkv = bank()
                            P.op("pe", MM(pkv[:, 0:256], kendT[c2 * 64:(c2 + 1) * 64, p, :], vtok[c2 * 64:(c2 + 1) * 64, p, :], True, True), [kendT, vtok], [pkv])
                            P.op("dve", STT(Sst[:], Sst[:], decay[:, 2 * p + c2:2 * p + c2 + 1], pkv[:, 0:256], ALU.mult, ALU.add), [Sst, decay, pkv], [Sst])
                            sbi += 1
                            sb_new = Sbs[sbi % 4]
                            P.op("act", lambda e, sb_new=sb_new: e.copy(sb_new[:], Sst[:]), [Sst], [sb_new])
                            if c2 == 0:
                                for half in range(2):
                                    po = pos_[half]
                                    P.op("pe", MM(po[:, 64:128], sb_new[:, half * 128:(half + 1) * 128], qdec[:, p * 128 + 64:(p + 1) * 128], False, True), [sb_new, qdec], [po])
                        o_ = ost[osi % 3]
                        osi += 1
                        for half in range(2):
                            P.op("act", lambda e, po=pos_[half], o_=o_, half=half: e.copy(o_[:, half, :], po[:, 0:128]), [pos_[half]], [o_])
                        P.dma("sp", DMA(ol_d.h.ap()[hh * 256:(hh + 1) * 256, p * 128:(p + 1) * 128].rearrange("(h p) q -> p h q", p=128), o_[:]), [o_], [ol_d])
                    P.dma("sp", DMA(finsend[:, 32 + hh * 256:32 + (hh + 1) * 256], Sst[:]), [Sst], [finsend])
                P.dma("sp", DMA(finsend[:, 16:32], dt4[:]), [dt4], [finsend])
                P.barrier(coll=False)

            P.coll(lambda e: e.collective_compute("AllGather", ALU.bypass, replica_groups=[list(range(NCORE))],
                                                         ins=[finsend.h.ap().opt()], outs=[finrecv.h.ap().opt()]), [finsend], [finrecv], key="fin")

            with ExitStack() as es:
                ohI = P.sb(es, "ohI", [128, 8, 128], BF16)
                slots = [P.sb(es, f"slot{i}", [128, 7, 2048], BF16) for i in range(2)]
                selo = [P.sb(es, f"selo{i}", [128, 2048], BF16) for i in range(2)]
                for j in range(7):
                    P.op("dve", TS(ohI[:, j, :], ident, flags[:, 16 + j:17 + j], None, ALU.mult), [cbf, flags], [ohI])
                nit = (KVW + 2047) // 2048
                for it in range(nit):
                    c0 = it * 2048
                    w = min(2048, KVW - c0)
                    sl_ = slots[it % 2]
                    so = selo[it % 2]
                    P.dma("sp", DMA(sl_[:, :, 0:w], kvrecv.h.ap()[0:7 * 128, c0:c0 + w].rearrange("(r p) n -> p r n", p=128)), [kvrecv], [sl_])
                    for q in range(w // 512):
                        pb = bank()
                        for j in range(7):
                            P.op("pe", MM(pb[:, :], ohI[:, j, :], sl_[:, j, q * 512:(q + 1) * 512], j == 0, j == 6), [ohI, sl_], [pb])
                        P.op("act", lambda e, so=so, pb=pb, q=q: e.copy(so[:, q * 512:(q + 1) * 512], pb[:, :]), [pb], [so])
                    P.dma("sp", DMA(kvprev[:, c0:c0 + w], so[:, 0:w]), [so], [kvprev])
            P.barrier(coll=False)

            def merge_branch(es, S, kb, yT_, nk, projd, tag):
                stg = [P.sb(es, f"stg{tag}{i}", [128, TT], F32) for i in range(3)]
                sgm = [P.sb(es, f"sgm{tag}{i}", [128, TT], F32) for i in range(2)]
                cnt = 0
                for och in range(2):
                    pw = S.load(projd.h.ap()[l, :, och * 512:(och + 1) * 512], nk=nk)
                    mgw = S.load(w_in_d.h.ap()[l, :, O_MG + kb * 1024 + och * 512:O_MG + kb * 1024 + (och + 1) * 512])
                    for oc4 in range(4):
                        oc = och * 4 + oc4
                        for tt in range(NT):
                            sl = slice(tt * TT, (tt + 1) * TT)
                            pp = bank()
                            for kc in range(nk):
                                P.op("pe", MM(pp[:, :], pw[:, kc, oc4 * 128:(oc4 + 1) * 128], yT_[:, kc, sl], kc == 0, kc == nk - 1), [pw, yT_], [pp])
                            pg = bank()
                            proj(pg[:, :], mgw, oc4 * 128, 128, tt, pg)
                            sm = sgm[cnt % 2]
                            st = stg[cnt % 3]
                            cnt += 1
                            P.op("act", ACT(sm[:], pg[:, :], AF.Sigmoid), [pg], [sm])
                            P.op("dve", TT_(st[:], pp[:, :], sm[:], ALU.mult), [pp, sm], [st])
                            P.dma("sp", DMA(tk_d[kb][oc * 128:(oc + 1) * 128, sl], st[:]), [st], [tk_d[kb]])

            with ExitStack() as es:
                S = Slabs(es, nst=1, nbf=3)
                qt = [P.sb(es, f"qt{i}", [128, T], BF16) for i in range(1)]
                kt = [P.sb(es, f"kt{i}", [128, 2 * T], BF16) for i in range(1)]
                vt = [P.sb(es, f"vt{i}", [128, 32, 128], BF16) for i in range(1)]
                acc = P.sb(es, "acc", [128, 2, T], F32)
                pT = [P.sb(es, f"pT{i}", [128, 256], BF16) for i in range(3)]
                rec = P.sb(es, "rec", [128, T], F32)
                sgb = P.sb(es, "sgb", [128, TT], F32)
                ybT = P.sb(es, "ybT", [128, 4, T], BF16)
                wbg = S.load(w_in_d.h.ap()[l, :, O_BG:O_BG + 512])
                ci = 0
                pti = 0
                for hp in range(4):
                    P.op("pool", lambda e: e.memset(acc[:], 0.0), [], [acc])
                    for g in range(3):
                        d = DILS[g]
                        q_, k_, v_ = qt[0], kt[0], vt[0]
                        ci += 1
                        P.dma("sp", DMA(q_[:], q_d[g * 4 + hp]), [q_d], [q_])
                        ok_, ov_ = kv_off(g, hp, 0), kv_off(g, hp, 1)
                        P.dma("sp", [DMA(k_[:, T:2 * T], k_d[g * 4 + hp]), DMA(k_[:, T - 128 * d:T], kvprev[:, ok_:ok_ + 128 * d])],
                              [k_d, kvprev], [k_])
                        P.dma("act", [DMA(v_[:, 16:32, :], v_d[g * 4 + hp].rearrange("p (a b) -> p a b", b=128)),
                                      DMA(v_[:, 16 - d:16, :].rearrange("p a b -> p (a b)"), kvprev[:, ov_:ov_ + 128 * d])],
                              [v_d, kvprev], [v_])
                        for blk in range(16):
                            n_, r_ = blk // d, blk % d
                            t0 = n_ * 128 * d + r_
                            qs = lambda hb: q_[hb:hb + 64, t0:t0 + 127 * d + 1:d]
                            kcur = lambda hb: k_[hb:hb + 64, T + t0:T + t0 + 127 * d + 1:d]
                            kprev = lambda hb: k_[hb:hb + 64, T + t0 - 128 * d:T + t0 - d + 1:d]
                            nm = negm[:, 256:512] if n_ == 0 else negm[:, 0:256]
                            for h in range(2):
                                hb = 64 * h
                                psS = bank()
                                P.op("pe", MM(psS[:, 0:256], ident, nm, True, False), [cbf, negm], [psS])
                                P.op("pe", MM(psS[:, 0:128], kprev(hb), qs(hb), False, False), [k_, q_], [psS])
                                P.op("pe", MM(psS[:, 128:256], kcur(hb), qs(hb), False, True), [k_, q_], [psS])
                                p_ = pT[pti % 3]
                                pti += 1
                                P.op("act", ACT(p_[:], psS[:, 0:256], AF.Exp), [psS], [p_])
                                psO = bank()
                                P.op("pe", MM(psO[:, 0:128], v_[:, 16 + blk - d, :], p_[:, 0:128], True, False), [v_, p_], [psO])
                                P.op("pe", MM(psO[:, 0:128], v_[:, 16 + blk, :], p_[:, 128:256], False, True), [v_, p_], [psO])
                                P.op("pe", MM(psO[:, 128:256], onesb, p_[:, 0:128], True, False), [cbf, p_], [psO])
                                P.op("pe", MM(psO[:, 128:256], onesb, p_[:, 128:256], False, True), [cbf, p_], [psO])
                                av = acc[hb:hb + 64, :, t0:t0 + 127 * d + 1:d]
                                P.op("dve", TT_(av, av, psO[hb:hb + 64, 0:256].rearrange("p (a b) -> p a b", b=128), ALU.add), [acc, psO], [acc])
                    P.op("dve", lambda e: e.reciprocal(rec[:], acc[:, 1, :]), [acc], [rec])
                    P.op("dve", TT_(rec[:], rec[:], acc[:, 0, :], ALU.mult), [rec, acc], [rec])
                    for tt in range(NT):
                        sl = slice(tt * TT, (tt + 1) * TT)
                        pg = bank()
                        proj(pg[:, :], wbg, hp * 128, 128, tt, pg)
                        P.op("act", ACT(sgb[:], pg[:, :], AF.Silu), [pg], [sgb])
                        P.op("dve", TT_(ybT[:, hp, sl], rec[:, sl], sgb[:], ALU.mult), [rec, sgb], [ybT])
                if debug and l == 0:
                    for k in range(4):
                        P.dma("sp", DMA(dbg["ybT"][k * 128:(k + 1) * 128, :], ybT[:, k, :]), [ybT], [dbg["ybT"]])
                merge_branch(es, S, 1, ybT, 4, proj_b_d, "b")
                P.barrier(coll=False)

            with ExitStack() as es:
                S = Slabs(es, nst=1, nbf=3)
                finr = P.sb(es, "finr", [128, 8, 32], F32)
                hin = P.sb(es, "hin", [128, 8], F32)
                tA = P.sb(es, "tA", [128, 8], F32)
                yaT = P.sb(es, "yaT", [128, 8, T], BF16)
                yl = [P.sb(es, f"yl{i}", [128, T], BF16) for i in range(2)]
                al = [P.sb(es, f"al{i}", [128, T], BF16) for i in range(2)]
                for j in range(NCORE):
                    P.dma("sp", DMA(finr[:, j, :], finrecv[j * 128:(j + 1) * 128, 0:32]), [finrecv], [finr])
                P.op("pool", lambda e: e.memset(hin[:], 0.0), [], [hin])
                for j in range(NCORE):
                    selj = flags[:, 8 + j:9 + j]
                    P.op("dve", TS(tA[:], finr[:, j, 8:16], -1.0, selj, ALU.add, ALU.mult), [finr, flags], [tA])
                    P.op("dve", TS(tA[:], tA[:], 1.0, None, ALU.add), [tA], [tA])
                    P.op("dve", TT_(hin[:], hin[:], tA[:], ALU.mult), [hin, tA], [hin])
                    P.op("dve", STT(hin[:], finr[:, j, 0:8], selj, hin[:], ALU.mult, ALU.add), [finr, flags, hin], [hin])
                if debug and l == 0:
                    P.dma("sp", DMA(dbg["hin"][:, :], hin[:]), [hin], [dbg["hin"]])
                    P.dma("sp", DMA(dbg["finr"][:, :], finr[:].rearrange("p r n -> p (r n)")), [finr], [dbg["finr"]])
                for cc in range(8):
                    y_, a_ = yl[cc % 2], al[cc % 2]
                    P.dma("sp", DMA(y_[:], ya_d[cc * 128:(cc + 1) * 128, :]), [ya_d], [y_])
                    P.dma("sp", DMA(a_[:], ag_d[cc * 128:(cc + 1) * 128, :]), [ag_d], [a_])
                    P.op("dve", STT(yaT[:, cc, :], a_[:], hin[:, cc:cc + 1], y_[:], ALU.mult, ALU.add), [a_, hin, y_], [yaT])
                if debug and l == 0:
                    for k in range(8):
                        P.dma("sp", DMA(dbg["yaT"][k * 128:(k + 1) * 128, :], yaT[:, k, :]), [yaT], [dbg["yaT"]])
                merge_branch(es, S, 0, yaT, 8, proj_a_d, "a")
                P.barrier()

            with ExitStack() as es:
                S = Slabs(es, nst=1, nbf=4)
                finr = P.sb(es, "finrc", [128, 8, 32], F32)
                tD = P.sb(es, "tD", [128, 8, 4], F32)
                Sin = P.sb(es, "Sin", [128, 256], F32)
                Sj = [P.sb(es, f"Sj{i}", [128, 256], F32) for i in range(2)]
                Sib = P.sb(es, "Sib", [128, 256], BF16)
                qg = P.sb(es, "qg", [128, T], BF16)
                ol = P.sb(es, "olc", [128, 2, T], F32)
                sq2_r = Ring(es, "sq2", [128, 2, TT], F32, 2)
                rs2_r = Ring(es, "rs2", [128, TT], F32, 2)
                sgc_r = Ring(es, "sgc", [128, TT], F32, 3)
                yn_r = Ring(es, "yn", [128, TT], F32, 3)
                ycT = P.sb(es, "ycT", [128, 8, T], BF16)
                for j in range(NCORE):
                    P.dma("sp", DMA(finr[:, j, :], finrecv[j * 128:(j + 1) * 128, 0:32]), [finrecv], [finr])
                for j in range(NCORE):
                    selj = flags[:, 8 + j:9 + j]
                    P.op("dve", TS(tD[:, j, :], finr[:, j, 16:20], -1.0, selj, ALU.add, ALU.mult), [finr, flags], [tD])
                    P.op("dve", TS(tD[:, j, :], tD[:, j, :], 1.0, None, ALU.add), [tD], [tD])
                wcg = [S.load(w_in_d.h.ap()[l, :, O_CG + i * 512:O_CG + (i + 1) * 512]) for i in range(2)]
                for hh in range(4):
                    P.op("pool", lambda e: e.memset(Sin[:], 0.0), [], [Sin])
                    for j in range(NCORE):
                        sj = Sj[j % 2]
                        P.dma("sp", DMA(sj[:], finrecv[j * 128:(j + 1) * 128, 32 + hh * 256:32 + (hh + 1) * 256]), [finrecv], [sj])
                        P.op("dve", TS(Sin[:], Sin[:], tD[:, j, hh:hh + 1], None, ALU.mult), [Sin, tD], [Sin])
                        P.op("dve", STT(Sin[:], sj[:], flags[:, 8 + j:9 + j], Sin[:], ALU.mult, ALU.add), [sj, flags, Sin], [Sin])
                    P.op("act", lambda e: e.copy(Sib[:], Sin[:]), [Sin], [Sib])
                    P.dma("sp", DMA(qg[:], qg_d[hh * 128:(hh + 1) * 128, :]), [qg_d], [qg])
                    for half in range(2):
                        P.dma("sp", DMA(ol[:, half, :], ol_d[(hh * 2 + half) * 128:(hh * 2 + half + 1) * 128, :]), [ol_d], [ol])
                    for tt in range(NT):
                        sl = slice(tt * TT, (tt + 1) * TT)
                        for half in range(2):
                            pc = bank()
                            P.op("pe", MM(pc[:, :], Sib[:, half * 128:(half + 1) * 128], qg[:, sl], True, True), [Sib, qg], [pc])
                            P.op("dve", TT_(ol[:, half, sl], ol[:, half, sl], pc[:, :], ALU.add), [ol, pc], [ol])
                        sq2, rs2 = sq2_r.next(), rs2_r.next()
                        P.op("act", ACT(sq2[:], ol[:, :, sl], AF.Square), [ol], [sq2])
                        pn = bank()
                        for half in range(2):
                            P.op("pe", MM(pn[:, :], mean256, sq2[:, half, :], half == 0, half == 1), [cf, sq2], [pn])
                        P.op("act", ACT(rs2[:], pn[:, :], AF.Ln, bias=cf[:, 386:387]), [pn, cf], [rs2])
                        P.op("act", ACT(rs2[:], rs2[:], AF.Exp, scale=-0.5), [rs2], [rs2])
                        for half in range(2):
                            ch = hh * 2 + half
                            pg = bank()
                            sgc, yn = sgc_r.next(), yn_r.next()
                            proj(pg[:, :], wcg[ch // 4], (ch % 4) * 128, 128, tt, pg)
                            P.op("act", ACT(sgc[:], pg[:, :], AF.Silu), [pg], [sgc])
                            P.op("dve", STT(yn[:], ol[:, half, sl], vec[:, l, 82 + half:83 + half], rs2[:], ALU.mult, ALU.mult), [ol, vec, rs2], [yn])
                            P.op("pool", TT_(ycT[:, ch, sl], yn[:], sgc[:], ALU.mult), [yn, sgc], [ycT])
                if debug and l == 0:
                    for k in range(8):
                        P.dma("sp", DMA(dbg["ycT"][k * 128:(k + 1) * 128, :], ycT[:, k, :]), [ycT], [dbg["ycT"]])
                merge_branch(es, S, 2, ycT, 8, proj_c_d, "c")
                P.barrier()

            with ExitStack() as es:
                S = Slabs(es, nst=1, nbf=2)
                wo = [S.load(w_o_d.h.ap()[l, :, i * 512:(i + 1) * 512]) for i in range(2)]
                tks = [[P.sb(es, f"tk{i}{j}", [128, TT], F32) for j in range(3)] for i in range(2)]
                mb = [P.sb(es, f"mb{i}", [128, 8, TT], BF16) for i in range(2)]
                xr = [P.sb(es, f"xr{i}", [128, TT], F32) for i in range(3)]
                xhs = P.sb(es, "xhs", [128, 8, 4], F32)
                if l == 0:
                    P.op("pool", lambda e: e.memset(xhs[:], 0.0), [], [xhs])
                ti = 0
                xi = 0
                for tt in range(NT):
                    sl = slice(tt * TT, (tt + 1) * TT)
                    m_ = mb[tt % 2]
                    for kc in range(8):
                        t3 = tks[ti % 2]
                        ti += 1
                        for b3 in range(3):
                            P.dma("sp" if b3 < 2 else "act", DMA(t3[b3][:], tk_d[b3][kc * 128:(kc + 1) * 128, sl]), [tk_d[b3]], [t3[b3]])
                        P.op("pool", TT_(t3[0][:], t3[0][:], t3[1][:], ALU.add), [t3[0], t3[1]], [t3[0]])
                        P.op("pool", TT_(m_[:, kc, :], t3[0][:], t3[2][:], ALU.add), [t3[0], t3[2]], [m_])
                    for oc in range(8):
                        pb = bank()
                        for kc in range(8):
                            P.op("pe", MM(pb[:, :], wo[oc // 4][:, kc, (oc % 4) * 128:(oc % 4 + 1) * 128], m_[:, kc, :], kc == 0, kc == 7), [wo[oc // 4], m_], [pb])
                        x_ = xr[xi % 3]
                        xi += 1
                        P.dma("act", DMA(x_[:], xsrc[oc * 128:(oc + 1) * 128, sl]), [xsrc], [x_])
                        P.op("dve", STT(x_[:], pb[:, :], der[:, l, 16 + oc:17 + oc], x_[:], ALU.mult, ALU.add), [pb, der, x_], [x_])
                        P.dma("sp", DMA(xdst[oc * 128:(oc + 1) * 128, sl], x_[:]), [x_], [xdst])
                        if l == 0 and tt == NT - 1:
                            P.op("dve", CP(xhs[:, oc, 0:3], x_[:, TT - 3:TT]), [x_], [xhs])
                if l == 0:
                    P.dma("sp", DMA(xhsend.h.ap().rearrange("p (k n) -> p k n", n=4), xhs[:]), [xhs], [xhsend])
                P.barrier()
            if l == 0:
                P.coll(lambda e: e.collective_compute("AllGather", ALU.bypass, replica_groups=[list(range(NCORE))],
                                                             ins=[xhsend.h.ap().opt()], outs=[xhrecv.h.ap().opt()]), [xhsend], [xhrecv], key="xh")
                P.barrier()
            if debug == "L0":
                break
        P.barrier()
    P.emit()
    return nc


def _consts():
    bf = ml_dtypes.bfloat16
    cbf = np.zeros((128, 640 + 2 * T), np.float32)
    cbf[:, 0:128] = np.eye(128)
    m = np.arange(128)
    partner = np.where(m % 64 < 32, m + 32, m - 32)
    rp = np.zeros((128, 128), np.float32)
    rp[partner, m] = 1.0
    cbf[:, 128:256] = rp
    s = np.arange(128)[:, None]
    q = np.arange(128)[None, :]
    cbf[:, 256:384] = ((s // 64 == q // 64) & (s % 64 <= q % 64)).astype(np.float32)
    cbf[:, 384:640] = 1.0
    rm = np.ones(T, np.float32)
    rm[::64] = 0.0
    cbf[:, 640:640 + T] = rm[None, :]
    cbf[:, 640 + T:] = 1.0
    cf = np.zeros((128, 388), np.float32)
    cf[:, 0:128] = 1.0 / 1024
    blk = np.zeros((128, 128), np.float32)
    blk[:64, :64] = 1.0 / 64
    blk[64:, 64:] = 1.0 / 64
    cf[:, 128:256] = blk
    cf[:, 256:384] = 1.0 / 256
    half = 32
    inv = (np.float32(10000.0) ** (-(np.arange(half, dtype=np.float32) / np.float32(half)))).astype(np.float32)
    cf[:, 384] = inv[m % 32]
    cf[:, 385] = np.where(m % 64 < 32, -1.0, 1.0)
    cf[:, 386] = EPS
    return cbf.astype(bf), cf


def _negmask(first):
    bf = ml_dtypes.bfloat16
    kj = np.arange(128)[:, None]
    qi = np.arange(128)[None, :]
    NEG = -30000.0
    prev = np.where(kj >= qi, 0.0, NEG).astype(np.float32)
    cur = np.where(kj <= qi, 0.0, NEG).astype(np.float32)
    prev_first = np.full((128, 128), NEG, np.float32) if first else prev
    return np.concatenate([prev, cur, prev_first, cur], axis=1).astype(bf)


def _col(v):
    v = np.asarray(v, np.float32)
    return v.reshape(-1, 128).T


def make_in_maps(inputs):
    f32 = lambda a: np.ascontiguousarray(np.asarray(a, dtype=np.float32))
    x = f32(inputs["x"])
    c = f32(inputs["c"])
    pos = np.asarray(inputs["positions"]).astype(np.int32)
    cbf, cf = _consts()
    vec = np.zeros((DEPTH, 128, NV), np.float32)
    for l in range(DEPTH):
        vec[l, :, 0:8] = _col(inputs["norm_g"][l])
        vec[l, :, 8:16] = _col(inputs["conv_b"][l])
        cw = np.asarray(inputs["conv_w"][l], np.float32)
        for cc in range(8):
            for k in range(4):
                vec[l, :, 16 + cc * 4 + k] = cw[k, cc * 128:(cc + 1) * 128]
        vec[l, :, 48:56] = _col(inputs["lru_ba"][l])
        vec[l, :, 56:64] = _col(inputs["lru_bx"][l])
        vec[l, :, 64:72] = _col(inputs["lru_lambda"][l])
        vec[l, :, 72:75] = np.tile(np.asarray(inputs["qn_g"][l], np.float32).T, (2, 1))
        vec[l, :, 75:78] = np.tile(np.asarray(inputs["kn_g"][l], np.float32).T, (2, 1))
        vec[l, :, 78:82] = _col(inputs["gla_ab"][l])
        vec[l, :, 82:84] = _col(inputs["gla_on_g"][l])
        vec[l, :, 84:108] = _col(inputs["ada_b"][l])
    shared = {
        "cbf": cbf, "cf": cf, "vec": vec,
        "lru_wa": f32(inputs["lru_wa"]), "lru_wx": f32(inputs["lru_wx"]),
        "gla_a1": f32(inputs["gla_a1"]), "gla_a2": f32(inputs["gla_a2"]),
    }
    wl = [np.concatenate([f32(inputs[n][l]).reshape(-1) for n, _ in WSIZES]).reshape(NCORE, WROWS // DEPTH // NCORE, 2048)
          for l in range(DEPTH)]
    wsh = np.concatenate(wl, axis=1)
    in_maps = []
    for cidx in range(NCORE):
        b, qd = cidx // 4, cidx % 4
        sl = slice(qd * T, (qd + 1) * T)
        xT = np.ascontiguousarray(x[b, sl, :].T)
        xh = np.zeros((128, 8, 4), np.float32)
        if qd > 0:
            hx = x[b, qd * T - 3:qd * T, :]
            xh[:, :, 0:3] = hx.T.reshape(8, 128, 3).transpose(1, 0, 2)
        flags = np.zeros((128, 24), np.float32)
        flags[:, 0] = 1.0 if qd > 0 else 0.0
        if qd > 0:
            flags[:, 16 + cidx - 1] = 1.0
        for j in range(NCORE):
            flags[:, 8 + j] = 1.0 if (j // 4 == b and j < cidx) else 0.0
        m = dict(shared)
        m.update({
            "xT": xT, "xh0": xh.reshape(128, 32),
            "posb": np.ascontiguousarray(np.broadcast_to(pos[b, sl][None, :], (128, T))).astype(np.int32),
            "cT": np.ascontiguousarray(_col(c[b])),
            "flags": flags,
            "idxp": np.array([[max(cidx - 1, 0)]], np.int32),
            "negm": _negmask(qd == 0),
            "wshard": wsh[cidx],
        })
        in_maps.append(m)
    return in_maps


_NC_CACHE = {}


def kernel(**inputs):
    if "nc" not in _NC_CACHE:
        _NC_CACHE["nc"] = build_program()
    nc = _NC_CACHE["nc"]
    in_maps = make_in_maps(inputs)
    res = run_bass_kernel_spmd(nc, in_maps, core_ids=list(range(NCORE)))
    out = np.zeros((2, 4 * T, D), np.float32)
    for cidx in range(NCORE):
        b, qd = cidx // 4, cidx % 4
        out[b, qd * T:(qd + 1) * T, :] = np.asarray(res.results[cidx]["yT"]).T
    return out
```
